# Optimizing a Trainium2 kernel written in Bass

```python
import math
import jax, jax.numpy as jnp
from jax import lax
import numpy as np

D_MODEL = 4096
BATCH = 4
SEQ = 2048
DEPTH = 2
DEC_BATCH = 8
DEC_SEQ = 8
PAST_LEN = 16384
PAGE_SIZE = 128

N_MIXERS = 2
N_A_LAYERS = (DEPTH + 1) // 2
N_B_LAYERS = DEPTH // 2
EPS = 1e-6

A_HEADS = 32
A_HEAD_DIM = D_MODEL // A_HEADS
A_KV = 4
A_HPG = A_HEADS // A_KV
A_WIDTH = A_HEADS * A_HEAD_DIM
A_KV_WIDTH = A_KV * A_HEAD_DIM
CMP_STRIDE = 16
CMP_LEN = 2 * CMP_STRIDE
SEL_BLOCK = 64
SEL_RATIO = SEL_BLOCK // CMP_STRIDE
SEL_TOPK = 16
WINDOW = 512
Q_BLOCK = 128
FORCE_SCORE = 1e4
NEG = -1e30
TINY = 1e-30
A_SPLIT_SIZES = (A_WIDTH,) + (A_KV_WIDTH,) * 6 + (A_WIDTH,) * 3 + (3 * A_HEADS,)
A_IN = sum(A_SPLIT_SIZES)
A_SPLITS = tuple(int(v) for v in np.cumsum(A_SPLIT_SIZES)[:-1])

R_HEADS = 16
R_DK = D_MODEL // R_HEADS
R_DV = 2 * R_DK
R_KW = R_HEADS * R_DK
R_VW = R_HEADS * R_DV
R_IN = 2 * R_KW + 2 * R_VW
R_SPLITS = (R_KW, 2 * R_KW, 2 * R_KW + R_VW)
R_CHUNK = 128
ROPE_BASE = 10000.0

N_BUCKETS = 32
MAX_DISTANCE = 1024

kernel_name = 'nsa_retention_hybrid_step'


def rmsnorm(x, g=None):
    xf = x.astype(jnp.float32)
    y = xf * lax.rsqrt(jnp.mean(jnp.square(xf), axis=-1, keepdims=True) + EPS)
    if g is not None:
        y = y * g.astype(jnp.float32)
    return y.astype(x.dtype)


def ada_pre(x, c, g, w, b):
    shift, scale, gate = jnp.split(jax.nn.silu(c) @ w + b, 3, axis=-1)
    h = rmsnorm(x, g) * (1 + scale[:, None]) + shift[:, None]
    return h, gate[:, None]


def t5_bucket(dist):
    n = jnp.maximum(dist, 0)
    exact = N_BUCKETS // 2
    logv = jnp.log(jnp.maximum(n, 1).astype(jnp.float32) / exact) / math.log(MAX_DISTANCE / exact)
    large = jnp.minimum(exact + (logv * (N_BUCKETS - exact)).astype(jnp.int32), N_BUCKETS - 1)
    return jnp.where(n < exact, n, large)


def masked_softmax(s, mask):
    s = jnp.where(mask, s, NEG)
    m = jnp.max(s, axis=-1, keepdims=True)
    e = jnp.where(mask, jnp.exp(s - m), 0.0)
    return e / jnp.maximum(e.sum(-1, keepdims=True), TINY)


def rotary(x, pos):
    half = x.shape[-1] // 2
    inv = jnp.power(ROPE_BASE, -jnp.arange(half, dtype=jnp.float32) / half)
    ang = pos.astype(jnp.float32)[:, None] * inv[None, :]
    cos = jnp.cos(ang)[None, :, None, :]
    sin = jnp.sin(ang)[None, :, None, :]
    xf = x.astype(jnp.float32)
    x1, x2 = xf[..., :half], xf[..., half:]
    return jnp.concatenate([x1 * cos - x2 * sin, x1 * sin + x2 * cos], axis=-1)


def compress(x, pe, w1, w2):
    B, L, G, hd = x.shape
    n_ch = -(-L // CMP_STRIDE)
    x = jnp.pad(x, ((0, 0), (0, n_ch * CMP_STRIDE - L), (0, 0), (0, 0))).reshape(B, n_ch, CMP_STRIDE, G, hd)
    blk = jnp.concatenate([x[:, :-1], x[:, 1:]], axis=2) + pe[:, None, :]
    flat = jnp.swapaxes(blk, 2, 3).reshape(B, n_ch - 1, G, CMP_LEN * hd)
    return jax.nn.silu(flat @ w1) @ w2


def nsa_mixer(h, q0, past_kv, win_buf, rel_bias, w_in, w_out, cmp_pe, cmp_w1, cmp_w2, qk_g):
    B, T, _ = h.shape
    hd = A_HEAD_DIM
    scale = hd ** -0.5
    q, kc, vc, ks, vs, kw, vw, zc, zs, zw, gl = jnp.split(h @ w_in, A_SPLITS, axis=-1)
    q = rmsnorm(q.reshape(B, T, A_KV, A_HPG, hd), qk_g[0])
    kvh = lambda t: t.reshape(B, T, A_KV, hd)
    ks = rmsnorm(kvh(ks), qk_g[2])
    kw = rmsnorm(kvh(kw), qk_g[3])
    new_rows = jnp.stack([kvh(kc), kvh(vc), ks, kvh(vs)], axis=2)
    full = new_rows if past_kv is None else jnp.concatenate([past_kv, new_rows], axis=1)
    L = full.shape[1]
    kcmp = rmsnorm(compress(full[:, :, 0], cmp_pe[0], cmp_w1[0], cmp_w2[0]), qk_g[1])
    vcmp = compress(full[:, :, 1], cmp_pe[1], cmp_w1[1], cmp_w2[1])
    NC = kcmp.shape[1]
    cmp_end = jnp.arange(NC) * CMP_STRIDE + (CMP_LEN - 1)
    n_sel = -(-L // SEL_BLOCK)
    k_top = min(SEL_TOPK, n_sel)
    sel = jnp.pad(full[:, :, 2:4], ((0, 0), (0, n_sel * SEL_BLOCK - L), (0, 0), (0, 0), (0, 0)))
    sel = sel.reshape(B, n_sel, SEL_BLOCK, 2, A_KV, hd).transpose(3, 0, 4, 1, 2, 5)
    new_win = jnp.stack([kw, kvh(vw)], axis=2)
    if win_buf is None:
        win_all = new_win
        win_state = new_win[:, T - min(WINDOW, T):]
    else:
        win_all = jnp.concatenate([win_buf, new_win], axis=1)
        win_state = win_all[:, T:]
    n_pre = win_all.shape[1] - T
    QB = min(Q_BLOCK, T)
    n_blk = -(-T // QB)
    Tp = n_blk * QB
    win_all = jnp.pad(win_all, ((0, 0), (WINDOW - n_pre, Tp - T), (0, 0), (0, 0), (0, 0)))
    qpad = jnp.pad(q, ((0, 0), (0, Tp - T), (0, 0), (0, 0), (0, 0)))
    rb = rel_bias.reshape(N_BUCKETS, A_KV, A_HPG)
    bi = jnp.arange(B)[:, None, None, None]
    gi = jnp.arange(A_KV)[None, None, :, None]
    jj = jnp.arange(n_sel)

    def block(i0):
        qb = lax.dynamic_slice_in_dim(qpad, i0, QB, axis=1)
        qpos = q0 + i0 + jnp.arange(QB)
        s = jnp.einsum('bqghd,bcgd->bqghc', qb, kcmp, preferred_element_type=jnp.float32) * scale
        s = s + rb[t5_bucket(qpos[:, None] - cmp_end[None, :])].transpose(0, 2, 3, 1)
        p_c = masked_softmax(s, (cmp_end[None, :] <= qpos[:, None])[None, :, None, None, :])
        o_c = jnp.einsum('bqghc,bcgd->bqghd', p_c.astype(vcmp.dtype), vcmp)
        imp = p_c.sum(3)
        ip = jnp.pad(imp, ((0, 0), (0, 0), (0, 0), (1, SEL_RATIO * n_sel - NC)))
        imp_sel = (ip[..., 1:] + ip[..., :-1]).reshape(B, QB, A_KV, n_sel, SEL_RATIO).sum(-1)
        cur = (qpos // SEL_BLOCK)[:, None, None]
        forced = (jj == 0) | (jj == cur) | (jj == cur - 1)
        valid = jj * SEL_BLOCK <= qpos[:, None, None]
        score = jnp.where(valid, jnp.where(forced, FORCE_SCORE, imp_sel), -1.0)
        _, idx = lax.top_k(score, k_top)
        kb = sel[0][bi, gi, idx].reshape(B, QB, A_KV, k_top * SEL_BLOCK, hd)
        vb = sel[1][bi, gi, idx].reshape(B, QB, A_KV, k_top * SEL_BLOCK, hd)
        kpos = (idx[..., None] * SEL_BLOCK + jnp.arange(SEL_BLOCK)).reshape(B, QB, A_KV, k_top * SEL_BLOCK)
        dsel = qpos[None, :, None, None] - kpos
        s = jnp.einsum('bqghd,bqgnd->bqghn', qb, kb, preferred_element_type=jnp.float32) * scale
        s = s + jnp.moveaxis(rb[t5_bucket(dsel), gi], -1, 3)
        p_s = masked_softmax(s, (dsel >= 0)[:, :, :, None, :])
        o_s = jnp.einsum('bqghn,bqgnd->bqghd', p_s.astype(vb.dtype), vb)
        wk = lax.dynamic_slice_in_dim(win_all, i0, WINDOW + QB, axis=1)
        wpos = q0 - WINDOW + i0 + jnp.arange(WINDOW + QB)
        d = qpos[:, None] - wpos[None, :]
        wmask = (wpos[None, :] >= 0) & (d >= 0) & (d < WINDOW)
        s = jnp.einsum('bqghd,bkgd->bqghk', qb, wk[:, :, 0], preferred_element_type=jnp.float32) * scale
        s = s + rb[t5_bucket(d)].transpose(0, 2, 3, 1)
        p_w = masked_softmax(s, wmask[None, :, None, None, :])
        o_w = jnp.einsum('bqghk,bkgd->bqghd', p_w.astype(wk.dtype), wk[:, :, 1])
        return jnp.stack([o_c, o_s, o_w], axis=2)

    o = lax.map(block, jnp.arange(n_blk) * QB)
    o = jnp.moveaxis(o, 0, 1).reshape(B, Tp, 3, A_HEADS, hd)[:, :T]
    z = jnp.stack([zc, zs, zw], axis=2).reshape(B, T, 3, A_HEADS, hd)
    gates = jax.nn.sigmoid(gl.reshape(B, T, 3, A_HEADS, 1))
    mixed = (gates * o.astype(z.dtype) * jax.nn.silu(z)).sum(2).reshape(B, T, A_WIDTH)
    return mixed @ w_out, new_rows, win_state


def retention_mixer(h, q0, state0, w_in, w_out):
    B, T, _ = h.shape
    q, k, v, g = jnp.split(h @ w_in, R_SPLITS, axis=-1)
    pos = q0 + jnp.arange(T)
    q = rotary(q.reshape(B, T, R_HEADS, R_DK), pos)
    k = rotary(k.reshape(B, T, R_HEADS, R_DK), pos) * (R_DK ** -0.5)
    v = v.reshape(B, T, R_HEADS, R_DV).astype(jnp.float32)
    C = R_CHUNK if T % R_CHUNK == 0 else T
    n = T // C
    chunks = lambda t: t.reshape(B, n, C, R_HEADS, t.shape[-1]).transpose(1, 0, 3, 2, 4)
    log_g = jnp.log1p(-jnp.exp2(-5.0 - jnp.arange(R_HEADS, dtype=jnp.float32)))
    ii = jnp.arange(C, dtype=jnp.float32)
    rel = ii[:, None] - ii[None, :]
    causal = rel >= 0
    dmat = jnp.where(causal, jnp.exp(jnp.where(causal, rel, 0.0) * log_g[:, None, None]), 0.0)
    q_dec = jnp.exp((ii + 1.0) * log_g[:, None])
    k_dec = jnp.exp((C - 1.0 - ii) * log_g[:, None])
    c_dec = jnp.exp(C * log_g)

    def step(S, inp):
        qc, kc, vc = inp
        att = jnp.einsum('bhid,bhjd->bhij', qc, kc) * dmat
        o = jnp.einsum('bhij,bhjv->bhiv', att, vc) + jnp.einsum('bhid,bhdv->bhiv', qc, S) * q_dec[:, :, None]
        S = S * c_dec[:, None, None] + jnp.einsum('bhjd,bhjv->bhdv', kc * k_dec[:, :, None], vc)
        return S, o

    S, o = lax.scan(step, state0.astype(jnp.float32), (chunks(q), chunks(k), chunks(v)))
    o = o.transpose(1, 0, 3, 2, 4).reshape(B, T, R_HEADS, R_DV)
    o = rmsnorm(o).reshape(B, T, R_VW).astype(h.dtype)
    return (o * jax.nn.silu(g)) @ w_out, S.astype(state0.dtype)


def setup_inputs(seed: int = 0) -> dict:
    key = jax.random.key(seed)
    ks = jax.random.split(key, 20)
    n_pages = PAST_LEN // PAGE_SIZE
    n_pool = (5 * DEC_BATCH * n_pages + 3) // 4
    w_buf = min(WINDOW, PAST_LEN)
    nrm = lambda k, shape, s: s * jax.random.normal(k, shape, jnp.float32)
    page_table = jax.random.permutation(ks[7], n_pool)[: DEC_BATCH * n_pages].reshape(DEC_BATCH, n_pages).astype(jnp.int32)
    return {
        'x_prompt': nrm(ks[0], (BATCH, SEQ, D_MODEL), 1.0),
        'x_sample': nrm(ks[1], (DEC_BATCH, DEC_SEQ, D_MODEL), 1.0),
        'c_prompt': nrm(ks[2], (BATCH, D_MODEL), 1.0),
        'c_sample': nrm(ks[3], (DEC_BATCH, D_MODEL), 1.0),
        'cache_nsa_kv': nrm(ks[4], (N_A_LAYERS, n_pool, PAGE_SIZE, 4, A_KV, A_HEAD_DIM), 1.0),
        'cache_nsa_win': nrm(ks[5], (N_A_LAYERS, DEC_BATCH, w_buf, 2, A_KV, A_HEAD_DIM), 1.0),
        'state_ret': nrm(ks[6], (N_B_LAYERS, DEC_BATCH, R_HEADS, R_DK, R_DV), 0.1),
        'page_table': page_table,
        'norm_g': 1.0 + nrm(ks[8], (DEPTH, D_MODEL), 0.1),
        'ada_w': nrm(ks[9], (DEPTH, D_MODEL, 3 * D_MODEL), 0.5 * D_MODEL ** -0.5),
        'ada_b': nrm(ks[10], (DEPTH, 3 * D_MODEL), 0.02),
        'rel_bias': nrm(ks[11], (N_BUCKETS, A_HEADS), 0.3),
        'a_w_in': nrm(ks[12], (N_A_LAYERS, D_MODEL, A_IN), D_MODEL ** -0.5),
        'a_w_out': nrm(ks[13], (N_A_LAYERS, A_WIDTH, D_MODEL), A_WIDTH ** -0.5),
        'a_cmp_pe': nrm(ks[14], (N_A_LAYERS, 2, CMP_LEN, A_HEAD_DIM), 0.02),
        'a_cmp_w1': nrm(ks[15], (N_A_LAYERS, 2, CMP_LEN * A_HEAD_DIM, A_HEAD_DIM), (CMP_LEN * A_HEAD_DIM) ** -0.5),
        'a_cmp_w2': nrm(ks[16], (N_A_LAYERS, 2, A_HEAD_DIM, A_HEAD_DIM), A_HEAD_DIM ** -0.5),
        'a_qk_g': 1.0 + nrm(ks[17], (N_A_LAYERS, 4, A_HEAD_DIM), 0.1),
        'r_w_in': nrm(ks[18], (N_B_LAYERS, D_MODEL, R_IN), D_MODEL ** -0.5),
        'r_w_out': nrm(ks[19], (N_B_LAYERS, R_VW, D_MODEL), R_VW ** -0.5),
    }


def reference(x_prompt, x_sample, c_prompt, c_sample, cache_nsa_kv, cache_nsa_win, state_ret, page_table,
              norm_g, ada_w, ada_b, rel_bias, a_w_in, a_w_out, a_cmp_pe, a_cmp_w1, a_cmp_w2, a_qk_g,
              r_w_in, r_w_out):
    past_len = page_table.shape[1] * PAGE_SIZE
    xp, xs = x_prompt, x_sample
    kv_p, kv_s, win_p, win_s, ret_p, ret_s = [], [], [], [], [], []
    for layer in range(DEPTH):
        hp, gp = ada_pre(xp, c_prompt, norm_g[layer], ada_w[layer], ada_b[layer])
        hs, gs = ada_pre(xs, c_sample, norm_g[layer], ada_w[layer], ada_b[layer])
        li = layer // N_MIXERS
        if layer % N_MIXERS == 0:
            wa = (rel_bias, a_w_in[li], a_w_out[li], a_cmp_pe[li], a_cmp_w1[li], a_cmp_w2[li], a_qk_g[li])
            yp, rows_p, wst_p = nsa_mixer(hp, 0, None, None, *wa)
            past = cache_nsa_kv[li][page_table]
            past = past.reshape(past.shape[0], past_len, 4, A_KV, A_HEAD_DIM)
            ys, rows_s, wst_s = nsa_mixer(hs, past_len, past, cache_nsa_win[li], *wa)
            kv_p.append(rows_p)
            kv_s.append(rows_s)
            win_p.append(wst_p)
            win_s.append(wst_s)
        else:
            s0 = jnp.zeros((xp.shape[0], R_HEADS, R_DK, R_DV), jnp.float32)
            yp, sp = retention_mixer(hp, 0, s0, r_w_in[li], r_w_out[li])
            ys, ss = retention_mixer(hs, past_len, state_ret[li], r_w_in[li], r_w_out[li])
            ret_p.append(sp)
            ret_s.append(ss)
        xp = xp + gp * yp
        xs = xs + gs * ys
    return (xp, xs, jnp.stack(kv_p), jnp.stack(kv_s), jnp.stack(win_p), jnp.stack(win_s), jnp.stack(ret_p), jnp.stack(ret_s))
```

```python
import numpy as np
import ml_dtypes
from contextlib import ExitStack
import concourse.bass as bass
import concourse.mybir as mybir
from concourse.bass_utils import run_bass_kernel_spmd

F32 = mybir.dt.float32
BF16 = mybir.dt.bfloat16
I32 = mybir.dt.int32
AF = mybir.ActivationFunctionType
ALU = mybir.AluOpType
AX = mybir.AxisListType

D = 4096
SEQ = 2048
NT = 16
NS = 2
NTT = NT + NS
A_IN = 19552
R_IN = 24576
EPS = 1e-6
NEGM = -30000.0
OFF = 2048
TABL = 18944
DEBUG = False
STAGE = 9
RUN_S = True
SCUT = -1
N_ADA = 2


class Buf:
    def __init__(self, name):
        self.name = name
        self.w = {}
        self.r = {}
        self.dsem = None
        self.dcnt = 0
        self.wdma = False


class KB:
    def __init__(self, nc):
        self.nc = nc
        self.eng = {'pe': nc.tensor, 'act': nc.scalar, 'dve': nc.vector, 'pool': nc.gpsimd, 'sp': nc.sync}
        self.sem = {}
        self.cnt = {}
        self.nsem = 0
        for e in self.eng:
            self._newsem(e)
        self.waited = {}
        self.all_ev = {}
        self.free_dsems = []
        self.dbufs = []

    def _newsem(self, e):
        self.nsem += 1
        self.sem[e] = self.nc.alloc_semaphore(f"c_{e}_{self.nsem}")
        self.cnt[e] = 0

    def _wait(self, e, ev):
        for num, (sem, val, src) in ev.items():
            key = (e, num)
            if self.waited.get(key, 0) >= val:
                continue
            self.waited[key] = val
            self.eng[e].wait_ge(sem, val)

    def _deps(self, e, reads, writes, dma=False):
        for b in reads:
            self._wait(e, {k: v for k, v in b.w.items() if not (v[2] == e and e == 'pe')})
        for b in writes:
            if not (dma and b.wdma and not b.r):
                self._wait(e, {k: v for k, v in b.w.items() if not (v[2] == e and e == 'pe')})
            self._wait(e, {k: v for k, v in b.r.items() if v[2] != e})

    def _record(self, ev, reads, writes):
        num = ev[0].num
        self.all_ev[num] = ev
        for b in reads:
            b.r[num] = ev
        for b in writes:
            b.w = {num: ev}
            b.r = {}

    def op(self, e, fn, reads=(), writes=(), inc=True):
        self._deps(e, reads, writes)
        ins = fn(self.eng[e])
        if inc:
            if self.cnt[e] >= 30000:
                self._newsem(e)
            self.cnt[e] += 1
            ins.then_inc(self.sem[e], 1)
            ev = (self.sem[e], self.cnt[e], e)
        else:
            ev = (self.sem[e], self.cnt[e] + 1, e)
        self._record(ev, reads, writes)
        for b in writes:
            b.wdma = False
        return ins

    def dma(self, q, out, in_, owner, reads=(), writes=(), indirect_idx=None, **kw):
        self._deps(q, reads, writes, dma=True)
        if owner.dsem is None:
            if self.free_dsems:
                owner.dsem, owner.dcnt = self.free_dsems.pop()
            else:
                self.nsem += 1
                owner.dsem = self.nc.alloc_semaphore(f"d_{owner.name}_{self.nsem}")
                owner.dcnt = 0
            self.dbufs.append(owner)
        if indirect_idx is not None:
            ins = self.nc.gpsimd.indirect_dma_start(out=out, out_offset=None, in_=in_,
                                                    in_offset=bass.IndirectOffsetOnAxis(ap=indirect_idx, axis=0), **kw)
        else:
            ins = self.eng[q].dma_start(out=out, in_=in_, **kw)
        owner.dcnt += 16
        ins.then_inc(owner.dsem, 16)
        ev = (owner.dsem, owner.dcnt, 'dma')
        num = ev[0].num
        self.all_ev[num] = ev
        for b in reads:
            b.r[num] = ev
        for b in writes:
            if b.wdma and not b.r:
                b.w[num] = ev
            else:
                b.w = {num: ev}
                b.r = {}
            b.wdma = True
        return ins

    def barrier(self):
        for e in self.eng:
            self._wait(e, {k: v for k, v in self.all_ev.items() if v[2] != e})
        for b in self.dbufs:
            self.free_dsems.append((b.dsem, b.dcnt))
            b.dsem = None
        self.dbufs = []

    def finish(self):
        self._wait('sp', dict(self.all_ev))


INPUT_NAMES = []


def build_program():
    nc = bass.Bass("TRN2", target_bir_lowering=False)
    kb = KB(nc)
    INPUT_NAMES.clear()
    def dt_in(name, shape, dt=F32):
        INPUT_NAMES.append(name)
        return nc.dram_tensor(name, list(shape), dt, kind="ExternalInput")
    dt_out = lambda name, shape: nc.dram_tensor(name, list(shape), F32, kind="ExternalOutput")

    xp = dt_in("xp", [SEQ, D]); xs = dt_in("xs", [NS * 8, D]); cT = dt_in("cT", [128, 32, 1 + NS])
    ada_w = dt_in("ada_w", [N_ADA, D, 3 * D]); ada_bT = dt_in("ada_bT", [2, 128, 96]); norm_gT = dt_in("norm_gT", [2, 128, 32])
    w_in0 = dt_in("w_in0", [D, A_IN])
    qkg_bc = dt_in("qkg_bc", [128, 4, 128])
    ident_d = dt_in("ident", [128, 128]); jmat_d = dt_in("jmat", [128, 128])
    w_out0 = dt_in("w_out0", [D, D])
    rel_bias = dt_in("rel_bias", [32, 32])
    oh_c = dt_in("oh_c", [33, TABL], BF16); oh_w = dt_in("oh_w", [33, TABL], BF16)
    MT1 = dt_in("MT1", [127, 33], BF16); Eall = dt_in("Eall", [32, 16, 128], BF16)
    vnf_d = dt_in("vnf", [128, 16, 32]); vnfm1_d = dt_in("vnfm1", [128, 16, 32]); forced_d = dt_in("forced", [128, 16, 32])
    peT = dt_in("peT", [128, 2, 32]); cmp_w1 = dt_in("cmp_w1", [2, 4096, 128]); cmp_w2 = dt_in("cmp_w2", [2, 128, 128])
    tab_c = nc.dram_tensor("tab_c", [32, TABL], BF16); tab_w = nc.dram_tensor("tab_w", [32, TABL], BF16)
    qT_d = nc.dram_tensor("qT_d", [NTT, 128, D], BF16); qT_b = [Buf(f"qT_{t}") for t in range(NTT)]
    SK = "ExternalOutput" if DEBUG else "Internal"
    mixed_d = nc.dram_tensor("mixed_d", [NTT * 128, D], F32, kind=SK); mixed_b = [Buf(f"mixed_{t}") for t in range(NTT)]
    x1_d = nc.dram_tensor("x1_d", [NTT * 128, D], F32, kind=SK); x1_b = [Buf(f"x1_{t}") for t in range(NTT)]
    proj1 = nc.dram_tensor("proj1", [NTT * 128, R_IN], F32); proj1_b = [Buf(f"proj1_{t}") for t in range(NTT)]
    og_d = nc.dram_tensor("og_d", [NTT * 128, 2 * D], F32, kind=SK); og_b = [Buf(f"og_{t}") for t in range(NTT)]

    y_p = dt_out("y_p", [SEQ, D]); y_s = dt_out("y_s", [NS * 8, D])
    kvr_p = dt_out("kvr_p", [SEQ, 2048]); kvr_s = dt_out("kvr_s", [NS * 8, 2048])
    win_p = dt_out("win_p", [512, 1024]); win_s = dt_out("win_s", [NS * 512, 1024])
    ret_p = dt_out("ret_p", [16, 256, 512]); ret_s = dt_out("ret_s", [NS * 16, 256, 512])

    proj0 = nc.dram_tensor("proj0", [NTT * 128, A_IN], F32)
    proj0_b = [Buf(f"proj0_{t}") for t in range(NTT)]

    def sb(name, shape, dt=F32):
        return nc.alloc_sbuf_tensor(name, list(shape), dt)

    ident = sb("ident_sb", [128, 128]); b_ident = Buf("ident")
    jmat = sb("jmat_sb", [128, 128]); b_jmat = Buf("jmat")
    kb.dma('sp', ident[:], ident_d.ap(), b_ident, writes=[b_ident])
    kb.dma('sp', jmat[:], jmat_d.ap(), b_jmat, writes=[b_jmat])
    qkg = sb("qkg_sb", [128, 4, 128]); b_qkg = Buf("qkg")
    kb.dma('sp', qkg[:], qkg_bc.ap(), b_qkg, writes=[b_qkg])

    psum = [nc.alloc_psum_tensor(f"ps{i}", [128, 512], F32) for i in range(8)]
    b_ps = [Buf(f"ps{i}") for i in range(8)]

    modT = [sb(f"modT{l}", [128, 96, 1 + NS]) for l in range(2)]
    b_modT = [Buf(f"modT{l}") for l in range(2)]
    G1T = [sb(f"G1T{l}", [128, 32, 1 + NS]) for l in range(2)]
    b_G1T = [Buf(f"G1T{l}") for l in range(2)]
    with ExitStack() as es1:
        cTs = es1.enter_context(nc.sbuf_tensor("cT_sb", [128, 32, 1 + NS], F32))
        sil = es1.enter_context(nc.sbuf_tensor("sil", [128, 32, 1 + NS], F32))
        silb = es1.enter_context(nc.sbuf_tensor("silb", [128, 32, 1 + NS], BF16))
        adab = es1.enter_context(nc.sbuf_tensor("adab", [128, 2, 96], F32))
        ngT = es1.enter_context(nc.sbuf_tensor("ngT", [128, 2, 32], F32))
        aw0 = es1.enter_context(nc.sbuf_tensor("aw0", [128, 32, 512], BF16))
        aw1 = es1.enter_context(nc.sbuf_tensor("aw1", [128, 32, 512], BF16))
        b_cT = Buf("cT"); b_sil = Buf("sil"); b_silb = Buf("silb"); b_adab = Buf("adab"); b_ngT = Buf("ngT")
        aw = [aw0, aw1]; b_aw = [Buf("aw0"), Buf("aw1")]
        kb.dma('sp', cTs[:], cT.ap(), b_cT, writes=[b_cT])
        kb.dma('sp', adab[:], ada_bT.ap().rearrange("l p j -> p l j"), b_adab, writes=[b_adab])
        kb.dma('sp', ngT[:], norm_gT.ap().rearrange("l p k -> p l k"), b_ngT, writes=[b_ngT])
        kb.op('act', lambda e: e.activation(out=sil[:], in_=cTs[:], func=AF.Silu), reads=[b_cT], writes=[b_sil])
        kb.op('dve', lambda e: e.tensor_copy(out=silb[:], in_=sil[:]), reads=[b_sil], writes=[b_silb])
        it = 0
        for l in range(N_ADA):
            pb = psum[l]
            for ct in range(24):
                s = it % 2
                kb.dma('pool', aw[s][:], ada_w[l, :, ct * 512:(ct + 1) * 512].rearrange("(k p) c -> p k c", p=128),
                       b_aw[s], writes=[b_aw[s]])
                for cc in range(4):
                    j = ct * 4 + cc
                    for k in range(32):
                        kb.op('pe', lambda e, j=j, k=k, cc=cc, s=s, pb=pb: e.matmul(
                            pb[:, j * (1 + NS):(j + 1) * (1 + NS)], lhsT=aw[s][:, k, cc * 128:(cc + 1) * 128], rhs=silb[:, k, :],
                            start=(k == 0), stop=(k == 31)),
                            reads=[b_aw[s], b_silb], writes=[b_ps[l]], inc=(k == 31))
                it += 1
            kb.op('dve', lambda e, l=l, pb=pb: e.tensor_tensor(
                out=modT[l][:], in0=pb[:, 0:96 * (1 + NS)].rearrange("p (j r) -> p j r", r=1 + NS),
                in1=adab[:, l, :].unsqueeze(2).to_broadcast([128, 96, 1 + NS]), op=ALU.add),
                reads=[b_ps[l], b_adab], writes=[b_modT[l]])
            kb.op('dve', lambda e, l=l: e.scalar_tensor_tensor(
                out=G1T[l][:], in0=modT[l][:, 32:64, :], scalar=1.0,
                in1=ngT[:, l, :].unsqueeze(2).to_broadcast([128, 32, 1 + NS]), op0=ALU.add, op1=ALU.mult),
                reads=[b_modT[l], b_ngT], writes=[b_G1T[l]])
        kb.barrier()

    UID = [0]

    def x_tile_src(t):
        if t < 16:
            return xp[t * 128:(t + 1) * 128, :], 128
        return xs[(t - 16) * 8:(t - 15) * 8, :], 8

    def phase_mm(KC, CW, blocks, load_x, norm_layer, w_dram, n_cols, epi, extra=None):
        maxb = max(len(b) for b in blocks)
        UID[0] += 1; u = f"_{UID[0]}"
        with ExitStack() as es1:
            hT = es1.enter_context(nc.sbuf_tensor("hT" + u, [128, maxb, KC, 128], BF16))
            xt0 = es1.enter_context(nc.sbuf_tensor("xt0" + u, [128, KC * 128], F32))
            junk = es1.enter_context(nc.sbuf_tensor("junk" + u, [128, D if norm_layer is not None else 8], BF16))
            st = es1.enter_context(nc.sbuf_tensor("st" + u, [128, 4], F32))
            wt0 = es1.enter_context(nc.sbuf_tensor("wt0" + u, [128, KC, CW], BF16))
            wt1 = es1.enter_context(nc.sbuf_tensor("wt1" + u, [128, KC, CW], BF16))
            b_xt = Buf("xt0"); b_junk = Buf("junk"); b_st = Buf("st")
            wt = [wt0, wt1]; b_wt = [Buf("wt0"), Buf("wt1")]
            b_hT = [Buf(f"hT{i}") for i in range(maxb)]
            nct = (n_cols + CW - 1) // CW
            wi = 0; ei = 0
            for blk in blocks:
                for li, t in enumerate(blk):
                    r = 0 if t < 16 else 1 + (t - 16)
                    load_x(t, xt0, b_xt)
                    if norm_layer is not None:
                        kb.op('act', lambda e: e.activation(out=junk[:], in_=xt0[:], func=AF.Square, accum_out=st[:, 0:1]),
                              reads=[b_xt], writes=[b_junk, b_st])
                        kb.op('dve', lambda e: e.tensor_scalar(out=st[:, 1:2], in0=st[:, 0:1], scalar1=1.0 / D, scalar2=EPS,
                                                               op0=ALU.mult, op1=ALU.add), reads=[b_st], writes=[b_st])
                        kb.op('act', lambda e: e.activation(out=st[:, 3:4], in_=st[:, 1:2], func=AF.Sqrt), reads=[b_st], writes=[b_st])
                        kb.op('dve', lambda e: e.reciprocal(out=st[:, 2:3], in_=st[:, 3:4]), reads=[b_st], writes=[b_st])
                        kb.op('dve', lambda e: e.tensor_scalar(out=xt0[:], in0=xt0[:], scalar1=st[:, 2:3], scalar2=None, op0=ALU.mult),
                              reads=[b_st, b_xt], writes=[b_xt])
                    for k4 in range(KC // 4):
                        pi = k4 % 2
                        for kk in range(4):
                            k = k4 * 4 + kk
                            kb.op('pe', lambda e, k=k, kk=kk, pi=pi: e.transpose(
                                psum[pi][:, kk * 128:(kk + 1) * 128], xt0[:, k * 128:(k + 1) * 128], ident[:]),
                                reads=[b_xt, b_ident], writes=[b_ps[pi]], inc=(kk == 3))
                        if norm_layer is not None:
                            for kk in range(4):
                                k = k4 * 4 + kk
                                kb.op('act', lambda e, li=li, k=k, kk=kk, pi=pi, r=r: e.activation(
                                    out=hT[:, li, k, :], in_=psum[pi][:, kk * 128:(kk + 1) * 128], func=AF.Identity,
                                    scale=G1T[norm_layer][:, k, r:r + 1], bias=modT[norm_layer][:, k, r:r + 1]),
                                    reads=[b_ps[pi], b_G1T[norm_layer], b_modT[norm_layer]], writes=[b_hT[li]])
                        else:
                            eng = 'act' if k4 % 2 == 0 else 'dve'
                            dst = hT[:, li, k4 * 4:(k4 + 1) * 4, :]
                            srcp = psum[pi][:, 0:512].rearrange("p (k c) -> p k c", k=4)
                            if eng == 'act':
                                kb.op('act', lambda e, dst=dst, srcp=srcp: e.copy(out=dst, in_=srcp), reads=[b_ps[pi]], writes=[b_hT[li]])
                            else:
                                kb.op('dve', lambda e, dst=dst, srcp=srcp: e.tensor_copy(out=dst, in_=srcp), reads=[b_ps[pi]], writes=[b_hT[li]])
                for ct in range(nct):
                    c0 = ct * CW; cw = min(CW, n_cols - c0)
                    s = wi % 2; wi += 1
                    kb.dma('pool', wt[s][:, :, 0:cw], w_dram[:, c0:c0 + cw].rearrange("(k p) c -> p k c", p=128),
                           b_wt[s], writes=[b_wt[s]])
                    for li, t in enumerate(blk):
                        pi = 2 + (ei % 6); ei += 1
                        for k in range(KC):
                            kb.op('pe', lambda e, li=li, k=k, pi=pi, s=s, cw=cw: e.matmul(
                                psum[pi][:, 0:cw], lhsT=hT[:, li, k, :], rhs=wt[s][:, k, 0:cw], start=(k == 0), stop=(k == KC - 1)),
                                reads=[b_hT[li], b_wt[s]], writes=[b_ps[pi]], inc=(k == KC - 1))
                        epi(t, c0, cw, pi)
            kb.barrier()

    def load_x_layer0(t, xt0, b_xt):
        src, nrow = x_tile_src(t)
        if nrow < 128:
            kb.op('dve', lambda e: e.memset(xt0[:], 0.0), writes=[b_xt])
        kb.dma('sp', xt0[0:nrow, :], src, b_xt, writes=[b_xt])

    class StoreEpi:
        def __init__(self, es, out_dram, out_bufs, CW):
            self.out_dram = out_dram; self.out_bufs = out_bufs
            UID[0] += 1
            self.ev = [es.enter_context(nc.sbuf_tensor(f"sev{i}_{UID[0]}", [128, CW], F32)) for i in range(4)]
            self.b_ev = [Buf(f"sev{i}") for i in range(4)]
            self.i = 0
        def __call__(self, t, c0, cw, pi):
            es = self.i % 4; self.i += 1
            ev = self.ev[es]; bev = self.b_ev[es]
            if self.i % 2 == 0:
                kb.op('dve', lambda e: e.tensor_copy(out=ev[:, 0:cw], in_=psum[pi][:, 0:cw]), reads=[b_ps[pi]], writes=[bev])
            else:
                kb.op('act', lambda e: e.copy(out=ev[:, 0:cw], in_=psum[pi][:, 0:cw]), reads=[b_ps[pi]], writes=[bev])
            kb.dma('sp', self.out_dram[t * 128:(t + 1) * 128, c0:c0 + cw], ev[:, 0:cw], bev, reads=[bev], writes=[self.out_bufs[t]])

    with ExitStack() as esA:
        epiA = StoreEpi(esA, proj0, proj0_b, 512)
        phase_mm(32, 512, [list(range(0, 8)) + [16], list(range(8, 16)) + [17]], load_x_layer0, 0, w_in0, A_IN, epiA)

    b_tab = Buf("tabs_dram")
    with ExitStack() as es2:
        rbs = es2.enter_context(nc.sbuf_tensor("rbs", [33, 32], F32))
        rbh = es2.enter_context(nc.sbuf_tensor("rbh", [33, 32], BF16))
        rbl = es2.enter_context(nc.sbuf_tensor("rbl", [33, 32], BF16))
        rbt = es2.enter_context(nc.sbuf_tensor("rbt", [33, 32], F32))
        oh = es2.enter_context(nc.sbuf_tensor("oh", [33, TABL], BF16))
        tabs = es2.enter_context(nc.sbuf_tensor("tabs", [32, TABL], BF16))
        b_rbs = Buf("rbs"); b_rbh = Buf("rbh"); b_rbl = Buf("rbl"); b_rbt = Buf("rbt"); b_oh = Buf("oh"); b_tabs = Buf("tabs")
        kb.op('dve', lambda e: e.memset(rbs[:], NEGM), writes=[b_rbs])
        kb.dma('sp', rbs[0:32, :], rel_bias.ap(), b_rbs, writes=[b_rbs])
        kb.op('dve', lambda e: e.tensor_copy(out=rbh[:], in_=rbs[:]), reads=[b_rbs], writes=[b_rbh])
        kb.op('dve', lambda e: e.tensor_copy(out=rbt[:], in_=rbh[:]), reads=[b_rbh], writes=[b_rbt])
        kb.op('dve', lambda e: e.tensor_tensor(out=rbt[:], in0=rbs[:], in1=rbt[:], op=ALU.subtract), reads=[b_rbs, b_rbt], writes=[b_rbt])
        kb.op('dve', lambda e: e.tensor_copy(out=rbl[:], in_=rbt[:]), reads=[b_rbt], writes=[b_rbl])
        for ohd, tabd in ((oh_c, tab_c), (oh_w, tab_w)):
            kb.dma('sp', oh[:], ohd.ap(), b_oh, writes=[b_oh])
            for ch in range(TABL // 512):
                pi = ch % 2
                kb.op('pe', lambda e, ch=ch, pi=pi: e.matmul(psum[pi][0:32, :], lhsT=rbh[:], rhs=oh[:, ch * 512:(ch + 1) * 512],
                                                            start=True, stop=False), reads=[b_rbh, b_oh], writes=[b_ps[pi]], inc=False)
                kb.op('pe', lambda e, ch=ch, pi=pi: e.matmul(psum[pi][0:32, :], lhsT=rbl[:], rhs=oh[:, ch * 512:(ch + 1) * 512],
                                                            start=False, stop=True), reads=[b_rbl, b_oh], writes=[b_ps[pi]])
                if ch % 2 == 0:
                    kb.op('act', lambda e, ch=ch, pi=pi: e.copy(out=tabs[:, ch * 512:(ch + 1) * 512], in_=psum[pi][0:32, :]),
                          reads=[b_ps[pi]], writes=[b_tabs])
                else:
                    kb.op('dve', lambda e, ch=ch, pi=pi: e.tensor_copy(out=tabs[:, ch * 512:(ch + 1) * 512], in_=psum[pi][0:32, :]),
                          reads=[b_ps[pi]], writes=[b_tabs])
            kb.dma('sp', tabd.ap(), tabs[:], b_tabs, reads=[b_tabs], writes=[b_tab])
        kb.barrier()
    def tab_ap(tab, off, pstep, npart):
        return bass.AP(tab, off, [[pstep, npart], [TABL, 8], [1, 128]])

    ksTn = [nc.alloc_sbuf_tensor(f"ksTn{i}", [128, 4, 128], BF16) for i in range(NS)]
    kwTn = [nc.alloc_sbuf_tensor(f"kwTn{i}", [128, 4, 128], BF16) for i in range(NS)]
    vsrn = [nc.alloc_sbuf_tensor(f"vsrn{i}", [128, 4, 129], BF16) for i in range(NS)]
    vwrn = [nc.alloc_sbuf_tensor(f"vwrn{i}", [128, 4, 129], BF16) for i in range(NS)]
    b_newkv = Buf("newkv"); b_outS = Buf("outS")
    for i in range(NS):
        kb.op('dve', lambda e, i=i: e.memset(vsrn[i][:, :, 128:129], 1.0), writes=[b_newkv])
        kb.op('dve', lambda e, i=i: e.memset(vwrn[i][:, :, 128:129], 1.0), writes=[b_newkv])
    with ExitStack() as es1:
        ksT = es1.enter_context(nc.sbuf_tensor("ksT", [128, 4, SEQ], BF16))
        kwT = es1.enter_context(nc.sbuf_tensor("kwT", [128, 4, SEQ], BF16))
        vsr = es1.enter_context(nc.sbuf_tensor("vsr", [128, 16, 4, 129], BF16))
        vwr = es1.enter_context(nc.sbuf_tensor("vwr", [128, 16, 4, 129], BF16))
        kcmpT = es1.enter_context(nc.sbuf_tensor("kcmpT", [128, 4, 128], BF16))
        vcmp = es1.enter_context(nc.sbuf_tensor("vcmp", [128, 4, 129], BF16))
        identb = es1.enter_context(nc.sbuf_tensor("identb", [128, 128], BF16))
        g0s = es1.enter_context(nc.sbuf_tensor("g0s", [128, 128], F32))
        es1b = ExitStack()
        kcT = es1b.enter_context(nc.sbuf_tensor("kcT", [128, 4, SEQ], BF16))
        vcT = es1b.enter_context(nc.sbuf_tensor("vcT", [128, 4, SEQ], BF16))
        b_ksT = Buf("ksT"); b_kwT = Buf("kwT"); b_kcT = Buf("kcT"); b_vcT = Buf("vcT"); b_vsr = Buf("vsr"); b_vwr = Buf("vwr")
        b_kcmpT = Buf("kcmpT"); b_vcmp = Buf("vcmp"); b_identb = Buf("identb"); b_g0s = Buf("g0s")
        kb.op('dve', lambda e: e.tensor_copy(out=identb[:], in_=ident[:]), reads=[b_ident], writes=[b_identb])
        kb.op('dve', lambda e: e.tensor_scalar(out=g0s[:], in0=qkg[:, 0, :], scalar1=128.0 ** -0.5, scalar2=None, op0=ALU.mult),
              reads=[b_qkg], writes=[b_g0s])
        kb.op('dve', lambda e: e.memset(vsr[:, :, :, 128:129], 1.0), writes=[b_vsr])
        kb.op('dve', lambda e: e.memset(vwr[:, :, :, 128:129], 1.0), writes=[b_vwr])
        kb.op('dve', lambda e: e.memset(vcmp[:, :, 128:129], 1.0), writes=[b_vcmp])

        with ExitStack() as es2:
            kvt0 = es2.enter_context(nc.sbuf_tensor("kvt0", [128, 3072], F32))
            kvt1 = es2.enter_context(nc.sbuf_tensor("kvt1", [128, 3072], F32))
            sq = es2.enter_context(nc.sbuf_tensor("sq", [128, 1024], F32))
            ss = es2.enter_context(nc.sbuf_tensor("ss", [128, 2, 8], F32))
            qt = es2.enter_context(nc.sbuf_tensor("qt", [128, D], F32))
            qsq = es2.enter_context(nc.sbuf_tensor("qsq", [128, D], F32))
            rq = es2.enter_context(nc.sbuf_tensor("rq", [128, 32], F32))
            qTt = es2.enter_context(nc.sbuf_tensor("qTt", [128, D], BF16))
            kvt = [kvt0, kvt1]; b_kvt = [Buf("kvt0"), Buf("kvt1")]; b_sq = Buf("sq"); b_ss = [Buf("ss0"), Buf("ss1")]
            b_qt = Buf("qt"); b_qsq = Buf("qsq"); b_rq = Buf("rq"); b_qTt = Buf("qTt")
            b_out = Buf("outs")
            pc = 0
            for t in range(NTT):
                s = t % 2
                kb.dma('sp', kvt[s][:], proj0[t * 128:(t + 1) * 128, 4096:7168], b_kvt[s], reads=[proj0_b[t]], writes=[b_kvt[s]])
                for wi_, (c0, gi) in enumerate(((1024, 2), (2048, 3))):
                    v3 = kvt[s][:, c0:c0 + 512].rearrange("p (g d) -> p g d", g=4)
                    sq3 = sq[:, wi_ * 512:(wi_ + 1) * 512].rearrange("p (g d) -> p g d", g=4)
                    kb.op('dve', lambda e, v3=v3, sq3=sq3: e.tensor_tensor(out=sq3, in0=v3, in1=v3, op=ALU.mult),
                          reads=[b_kvt[s]], writes=[b_sq])
                    kb.op('dve', lambda e, sq3=sq3, s=s, wi_=wi_: e.tensor_reduce(out=ss[:, s, wi_ * 4:(wi_ + 1) * 4], in_=sq3, axis=AX.X, op=ALU.add),
                          reads=[b_sq], writes=[b_ss[s]])
                kb.op('dve', lambda e, s=s: e.tensor_scalar(out=ss[:, s, :], in0=ss[:, s, :], scalar1=1.0 / 128, scalar2=EPS, op0=ALU.mult, op1=ALU.add),
                      reads=[b_ss[s]], writes=[b_ss[s]])
                kb.op('act', lambda e, s=s: e.activation(out=ss[:, s, :], in_=ss[:, s, :], func=AF.Sqrt), reads=[b_ss[s]], writes=[b_ss[s]])
                kb.op('dve', lambda e, s=s: e.reciprocal(out=ss[:, s, :], in_=ss[:, s, :]), reads=[b_ss[s]], writes=[b_ss[s]])
                for wi_, (c0, gi) in enumerate(((1024, 2), (2048, 3))):
                    v3 = kvt[s][:, c0:c0 + 512].rearrange("p (g d) -> p g d", g=4)
                    kb.op('dve', lambda e, v3=v3, s=s, wi_=wi_: e.tensor_tensor(
                        out=v3, in0=v3, in1=ss[:, s, wi_ * 4:(wi_ + 1) * 4].unsqueeze(2).to_broadcast([128, 4, 128]), op=ALU.mult),
                        reads=[b_kvt[s], b_ss[s]], writes=[b_kvt[s]])
                    kb.op('dve', lambda e, v3=v3, gi=gi: e.tensor_tensor(
                        out=v3, in0=v3, in1=qkg[:, gi, :].unsqueeze(1).to_broadcast([128, 4, 128]), op=ALU.mult),
                        reads=[b_kvt[s], b_qkg], writes=[b_kvt[s]])
                if t < 16:
                    kb.dma('sp', kvr_p[t * 128:(t + 1) * 128, :], kvt[s][:, 0:2048], b_kvt[s], reads=[b_kvt[s]], writes=[b_out])
                    if t >= 12:
                        kb.dma('sp', win_p[(t - 12) * 128:(t - 11) * 128, :], kvt[s][:, 2048:3072], b_kvt[s], reads=[b_kvt[s]], writes=[b_out])
                    rt = 15 - t
                    for c0, dst, bdst in ((0, kcT, b_kcT), (512, vcT, b_vcT), (1024, ksT, b_ksT), (2048, kwT, b_kwT)):
                        pi = pc % 4; pc += 1
                        for g in range(4):
                            kb.op('pe', lambda e, c0=c0, g=g, pi=pi, s=s: e.transpose(
                                psum[pi][:, g * 128:(g + 1) * 128], kvt[s][:, c0 + g * 128:c0 + (g + 1) * 128], jmat[:]),
                                reads=[b_kvt[s], b_jmat], writes=[b_ps[pi]], inc=(g == 3))
                        dv = dst[:, :, rt * 128:(rt + 1) * 128]
                        sv = psum[pi][:, 0:512].rearrange("p (g k) -> p g k", g=4)
                        if pc % 2 == 0:
                            kb.op('act', lambda e, dv=dv, sv=sv: e.copy(out=dv, in_=sv), reads=[b_ps[pi]], writes=[bdst])
                        else:
                            kb.op('dve', lambda e, dv=dv, sv=sv: e.tensor_copy(out=dv, in_=sv), reads=[b_ps[pi]], writes=[bdst])
                    for c0, dst, bdst in ((1536, vsr, b_vsr), (2560, vwr, b_vwr)):
                        pi = pc % 4; pc += 1
                        kb.op('pe', lambda e, c0=c0, pi=pi, s=s: e.matmul(psum[pi][:, 0:512], lhsT=jmat[:], rhs=kvt[s][:, c0:c0 + 512],
                                                                         start=True, stop=True),
                              reads=[b_kvt[s], b_jmat], writes=[b_ps[pi]])
                        dv = dst[:, rt, :, 0:128]
                        sv = psum[pi][:, 0:512].rearrange("p (g k) -> p g k", g=4)
                        kb.op('act', lambda e, dv=dv, sv=sv: e.copy(out=dv, in_=sv), reads=[b_ps[pi]], writes=[bdst])
                else:
                    si = t - 16
                    kb.dma('sp', kvr_s[si * 8:(si + 1) * 8, :], kvt[s][0:8, 0:2048], b_kvt[s], reads=[b_kvt[s]], writes=[b_out])
                    kb.dma('sp', win_s[si * 512 + 504:si * 512 + 512, :], kvt[s][0:8, 2048:3072], b_kvt[s], reads=[b_kvt[s]], writes=[b_out])
                    for c0, dst in ((1024, ksTn[si]), (2048, kwTn[si])):
                        pi = pc % 4; pc += 1
                        for g in range(4):
                            kb.op('pe', lambda e, c0=c0, g=g, pi=pi, s=s: e.transpose(
                                psum[pi][:, g * 128:(g + 1) * 128], kvt[s][:, c0 + g * 128:c0 + (g + 1) * 128], jmat[:]),
                                reads=[b_kvt[s], b_jmat], writes=[b_ps[pi]], inc=(g == 3))
                        sv = psum[pi][:, 0:512].rearrange("p (g k) -> p g k", g=4)
                        kb.op('act', lambda e, dst=dst, sv=sv: e.copy(out=dst[:], in_=sv), reads=[b_ps[pi]], writes=[b_newkv])
                    for c0, dst in ((1536, vsrn[si]), (2560, vwrn[si])):
                        pi = pc % 4; pc += 1
                        kb.op('pe', lambda e, c0=c0, pi=pi, s=s: e.matmul(psum[pi][:, 0:512], lhsT=jmat[:], rhs=kvt[s][:, c0:c0 + 512], start=True, stop=True),
                              reads=[b_kvt[s], b_jmat], writes=[b_ps[pi]])
                        sv = psum[pi][:, 0:512].rearrange("p (g k) -> p g k", g=4)
                        kb.op('act', lambda e, dst=dst, sv=sv: e.copy(out=dst[:, :, 0:128], in_=sv), reads=[b_ps[pi]], writes=[b_newkv])
                kb.dma('sp', qt[:], proj0[t * 128:(t + 1) * 128, 0:4096], b_qt, reads=[proj0_b[t]], writes=[b_qt])
                kb.op('act', lambda e: e.activation(out=qsq[:], in_=qt[:], func=AF.Square), reads=[b_qt], writes=[b_qsq])
                kb.op('dve', lambda e: e.tensor_reduce(out=rq[:], in_=qsq[:].rearrange("p (h d) -> p h d", h=32), axis=AX.X, op=ALU.add),
                      reads=[b_qsq], writes=[b_rq])
                kb.op('dve', lambda e: e.tensor_scalar(out=rq[:], in0=rq[:], scalar1=1.0 / 128, scalar2=EPS, op0=ALU.mult, op1=ALU.add),
                      reads=[b_rq], writes=[b_rq])
                kb.op('act', lambda e: e.activation(out=rq[:], in_=rq[:], func=AF.Sqrt), reads=[b_rq], writes=[b_rq])
                kb.op('dve', lambda e: e.reciprocal(out=rq[:], in_=rq[:]), reads=[b_rq], writes=[b_rq])
                q3 = qt[:].rearrange("p (h d) -> p h d", h=32)
                kb.op('dve', lambda e, q3=q3: e.tensor_tensor(out=q3, in0=q3, in1=rq[:].unsqueeze(2).to_broadcast([128, 32, 128]), op=ALU.mult),
                      reads=[b_qt, b_rq], writes=[b_qt])
                kb.op('dve', lambda e, q3=q3: e.tensor_tensor(out=q3, in0=q3, in1=g0s[:].unsqueeze(1).to_broadcast([128, 32, 128]), op=ALU.mult),
                      reads=[b_qt, b_g0s], writes=[b_qt])
                for h4 in range(8):
                    pi = 4 + (h4 % 4)
                    for kk in range(4):
                        hh = h4 * 4 + kk
                        kb.op('pe', lambda e, hh=hh, kk=kk, pi=pi: e.transpose(
                            psum[pi][:, kk * 128:(kk + 1) * 128], qt[:, hh * 128:(hh + 1) * 128], ident[:]),
                            reads=[b_qt, b_ident], writes=[b_ps[pi]], inc=(kk == 3))
                    if h4 % 2 == 0:
                        kb.op('act', lambda e, h4=h4, pi=pi: e.copy(out=qTt[:, h4 * 512:(h4 + 1) * 512], in_=psum[pi][:, 0:512]),
                              reads=[b_ps[pi]], writes=[b_qTt])
                    else:
                        kb.op('dve', lambda e, h4=h4, pi=pi: e.tensor_copy(out=qTt[:, h4 * 512:(h4 + 1) * 512], in_=psum[pi][:, 0:512]),
                              reads=[b_ps[pi]], writes=[b_qTt])
                kb.dma('sp', qT_d[t], qTt[:], b_qTt, reads=[b_qTt], writes=[qT_b[t]])
            kb.barrier()

        with ExitStack() as es3:
            w1s = es3.enter_context(nc.sbuf_tensor("w1s", [128, 2, 32, 128], BF16))
            w2s = es3.enter_context(nc.sbuf_tensor("w2s", [128, 2, 128], BF16))
            peTs = es3.enter_context(nc.sbuf_tensor("peTs", [128, 2, 32], F32))
            peTb = es3.enter_context(nc.sbuf_tensor("peTb", [128, 2, 32], BF16))
            pec = es3.enter_context(nc.sbuf_tensor("pec", [128, 2], F32))
            hid = es3.enter_context(nc.sbuf_tensor("hid", [128, 128], BF16))
            kcn = es3.enter_context(nc.sbuf_tensor("kcn", [128, 128], F32))
            cjunk = es3.enter_context(nc.sbuf_tensor("cjunk", [128, 128], F32))
            cst = es3.enter_context(nc.sbuf_tensor("cst", [128, 4], F32))
            b_w1s = Buf("w1s"); b_w2s = Buf("w2s"); b_peTs = Buf("peTs"); b_peTb = Buf("peTb"); b_pec = Buf("pec")
            b_hid = Buf("hid"); b_kcn = Buf("kcn"); b_cjunk = Buf("cjunk"); b_cst = Buf("cst")
            for kv in range(2):
                kb.dma('pool', w1s[:, kv, :, :], cmp_w1[kv].rearrange("(r d) j -> d r j", d=128), b_w1s, writes=[b_w1s])
                kb.dma('pool', w2s[:, kv, :], cmp_w2[kv], b_w2s, writes=[b_w2s])
            kb.dma('sp', peTs[:], peT.ap(), b_peTs, writes=[b_peTs])
            kb.op('dve', lambda e: e.tensor_copy(out=peTb[:], in_=peTs[:]), reads=[b_peTs], writes=[b_peTb])
            for kv in range(2):
                for r in range(32):
                    kb.op('pe', lambda e, kv=kv, r=r: e.matmul(psum[0][:, kv:kv + 1], lhsT=w1s[:, kv, r, :], rhs=peTb[:, kv, r:r + 1],
                                                              start=(r == 0), stop=(r == 31)),
                          reads=[b_w1s, b_peTb], writes=[b_ps[0]], inc=(r == 31))
            kb.op('dve', lambda e: e.tensor_copy(out=pec[:], in_=psum[0][:, 0:2]), reads=[b_ps[0]], writes=[b_pec])
            for kv in range(2):
                xT = kcT if kv == 0 else vcT
                b_xT = b_kcT if kv == 0 else b_vcT
                for g in range(4):
                    pi = 1 + (kv * 4 + g) % 2
                    for r in range(32):
                        kb.op('pe', lambda e, kv=kv, g=g, r=r, pi=pi, xT=xT: e.matmul(
                            psum[pi][:, 0:127], lhsT=w1s[:, kv, r, :], rhs=xT[:, g, (31 - r):(31 - r) + 2017:16],
                            start=(r == 0), stop=(r == 31)), reads=[b_w1s, b_xT], writes=[b_ps[pi]], inc=(r == 31))
                    kb.op('act', lambda e, kv=kv, pi=pi: e.activation(out=hid[:, 0:127], in_=psum[pi][:, 0:127], func=AF.Silu,
                                                                    bias=pec[:, kv:kv + 1]),
                          reads=[b_ps[pi], b_pec], writes=[b_hid])
                    po = 3 + (kv * 4 + g) % 2
                    kb.op('pe', lambda e, kv=kv, po=po: e.matmul(psum[po][0:127, 0:128], lhsT=hid[:, 0:127], rhs=w2s[:, kv, :],
                                                                start=True, stop=True), reads=[b_hid, b_w2s], writes=[b_ps[po]])
                    if kv == 0:
                        kb.op('act', lambda e, po=po: e.activation(out=cjunk[0:127, :], in_=psum[po][0:127, 0:128], func=AF.Square,
                                                                  accum_out=cst[0:127, 0:1]), reads=[b_ps[po]], writes=[b_cjunk, b_cst])
                        kb.op('dve', lambda e: e.tensor_scalar(out=cst[0:127, 1:2], in0=cst[0:127, 0:1], scalar1=1.0 / 128, scalar2=EPS,
                                                               op0=ALU.mult, op1=ALU.add), reads=[b_cst], writes=[b_cst])
                        kb.op('act', lambda e: e.activation(out=cst[0:127, 3:4], in_=cst[0:127, 1:2], func=AF.Sqrt), reads=[b_cst], writes=[b_cst])
                        kb.op('dve', lambda e: e.reciprocal(out=cst[0:127, 2:3], in_=cst[0:127, 3:4]), reads=[b_cst], writes=[b_cst])
                        kb.op('dve', lambda e, po=po: e.scalar_tensor_tensor(out=kcn[0:127, :], in0=psum[po][0:127, 0:128], scalar=cst[0:127, 2:3],
                                                                            in1=qkg[0:127, 1, :], op0=ALU.mult, op1=ALU.mult),
                              reads=[b_ps[po], b_cst, b_qkg], writes=[b_kcn])
                        pt_ = 5 + g % 2
                        kb.op('pe', lambda e, pt_=pt_: e.transpose(psum[pt_][:, 0:127], kcn[0:127, :], ident[0:127, 0:127]),
                              reads=[b_kcn, b_ident], writes=[b_ps[pt_]])
                        kb.op('dve', lambda e, pt_=pt_, g=g: e.tensor_copy(out=kcmpT[:, g, 0:127], in_=psum[pt_][:, 0:127]),
                              reads=[b_ps[pt_]], writes=[b_kcmpT])
                    else:
                        kb.op('dve', lambda e, po=po, g=g: e.tensor_copy(out=vcmp[0:127, g, 0:128], in_=psum[po][0:127, 0:128]),
                              reads=[b_ps[po]], writes=[b_vcmp])
            kb.barrier()
        if DEBUG:
            b_dbg2 = Buf('dbg2')
            for nm, tt, shp in (('kcmpT', kcmpT, [128, 4, 128]), ('vcmp', vcmp, [128, 4, 129]), ('ksT', ksT, [128, 4, SEQ]), ('vsr', vsr, [128, 16, 4, 129]), ('kcT', kcT, [128, 4, SEQ])):
                dd_ = nc.dram_tensor('dbg_' + nm, shp, BF16, kind='ExternalOutput')
                kb.dma('sp', dd_.ap(), tt[:], b_dbg2, reads=[b_kcmpT, b_vcmp, b_ksT, b_vsr, b_kcT], writes=[b_dbg2])
            kb.barrier()
        es1b.close()
        with ExitStack() as es4:
            SBc = lambda name, shape, dt=F32: es4.enter_context(nc.sbuf_tensor(name, list(shape), dt))
            qTs = [SBc(f"qTs{i}", [128, 1024], BF16) for i in range(2)]; b_qTs = [Buf(f"qTs{i}") for i in range(2)]
            z3 = SBc("z3", [128, 3, 1024]); b_z3 = Buf("z3")
            sz = SBc("sz", [128, 3, 1024]); b_sz = Buf("sz")
            gl = SBc("gl", [128, 96]); b_gl = Buf("gl")
            sig = SBc("sig", [128, 96]); b_sig = Buf("sig")
            Tc = SBc("Tc", [128, 1024], BF16); b_Tc = Buf("Tc")
            dsel = [SBc(f"dsel{i}", [128, 1024], BF16) for i in range(16)]; b_dsel = [Buf(f"dsel{i}") for i in range(16)]
            dwin = [SBc(f"dwin{i}", [128, 1024], BF16) for i in range(5)]; b_dwin = [Buf(f"dwin{i}") for i in range(5)]
            pt = [SBc(f"pt{i}", [128, 1024], BF16) for i in range(3)]; b_pt = [Buf(f"pt{i}") for i in range(3)]
            MT1s = SBc("MT1s", [128, 33], BF16); b_MT1s = Buf("MT1s")
            Es = SBc("Es", [32, 16, 128], BF16); b_Es = Buf("Es")
            vnfs = SBc("vnfs", [128, 16, 32]); vnfm1s = SBc("vnfm1s", [128, 16, 32]); forceds = SBc("forceds", [128, 16, 32])
            b_cst3 = Buf("cst3")
            rc = SBc("rc", [128, 8]); cc = SBc("cc", [128, 8]); tmp8 = SBc("tmp8", [128, 8, 32]); imps = SBc("imps", [128, 32])
            sc1 = SBc("sc1", [128, 32]); sc2 = SBc("sc2", [128, 32]); sc3 = SBc("sc3", [128, 32])
            mx = SBc("mx", [128, 8]); mx2 = SBc("mx2", [128, 8]); selm = SBc("selm", [128, 32])
            mbT4 = SBc("mbT4", [32, 4, 128], BF16); b_mbT4 = Buf("mbT4")
            tmpo = SBc("tmpo", [128, 8, 128]); b_tmpo = Buf("tmpo")
            mixed = SBc("mixed", [128, 1024]); b_mixed = Buf("mixed")
            b_sm = Buf("small")
            kb.dma('sp', MT1s[0:127, :], MT1.ap(), b_MT1s, writes=[b_MT1s])
            kb.dma('sp', Es[:], Eall.ap(), b_Es, writes=[b_Es])
            kb.dma('sp', vnfs[:], vnf_d.ap(), b_cst3, writes=[b_cst3])
            kb.dma('sp', vnfm1s[:], vnfm1_d.ap(), b_cst3, writes=[b_cst3])
            kb.dma('sp', forceds[:], forced_d.ap(), b_cst3, writes=[b_cst3])
            it = 0
            ptc = 0
            cur = {}
            b_dbg = Buf('dbg')
            if DEBUG:
                dbg_o = nc.dram_tensor('dbg_o', [3, SEQ, D], F32, kind='ExternalOutput')
                dbg_selm = nc.dram_tensor('dbg_selm', [SEQ, 4, 32], F32, kind='ExternalOutput')
                dbg_imps = nc.dram_tensor('dbg_imps', [SEQ, 4, 32], F32, kind='ExternalOutput')

            def branch_epilogue(br, g, first):
                for bi in range(3):
                    nh = 3 if bi < 2 else 2
                    view = psum[4 + bi][:, 0:nh * 129].rearrange("p (h c) -> p h c", c=129)
                    kb.op('dve', lambda e, bi=bi, nh=nh, view=view: e.reciprocal(out=rc[:, 3 * bi:3 * bi + nh].unsqueeze(2), in_=view[:, :, 128:129]),
                          reads=[b_ps[4 + bi]], writes=[b_sm])
                kb.op('dve', lambda e, br=br, g=g: e.tensor_tensor(out=cc[:], in0=rc[:], in1=sig[:, br * 32 + 8 * g:br * 32 + 8 * g + 8], op=ALU.mult),
                      reads=[b_sm, b_sig], writes=[b_sm])
                for bi in range(3):
                    nh = 3 if bi < 2 else 2
                    view = psum[4 + bi][:, 0:nh * 129].rearrange("p (h c) -> p h c", c=129)
                    kb.op('dve', lambda e, bi=bi, nh=nh, view=view: e.tensor_tensor(
                        out=tmpo[:, 3 * bi:3 * bi + nh, :], in0=view[:, :, 0:128],
                        in1=cc[:, 3 * bi:3 * bi + nh].unsqueeze(2).to_broadcast([128, nh, 128]), op=ALU.mult),
                        reads=[b_ps[4 + bi], b_sm], writes=[b_tmpo])
                fin_mix(br, first)

            def fin_mix(br, first):
                tf = tmpo[:].rearrange("p h d -> p (h d)")
                if DEBUG:
                    kb.dma('sp', dbg_o[br, cur['rows'], cur['g'] * 1024:(cur['g'] + 1) * 1024], tf, b_tmpo, reads=[b_tmpo], writes=[b_dbg])
                if first:
                    kb.op('dve', lambda e: e.tensor_tensor(out=mixed[:], in0=tf, in1=sz[:, br, :], op=ALU.mult),
                          reads=[b_tmpo, b_sz], writes=[b_mixed])
                else:
                    kb.op('dve', lambda e: e.tensor_tensor(out=tf, in0=tf, in1=sz[:, br, :], op=ALU.mult),
                          reads=[b_tmpo, b_sz], writes=[b_tmpo])
                    kb.op('dve', lambda e: e.tensor_tensor(out=mixed[:], in0=mixed[:], in1=tf, op=ALU.add),
                          reads=[b_tmpo, b_mixed], writes=[b_mixed])

            for g in range(4):
                for dd in range(16):
                    kb.dma('sp', dsel[dd][:].rearrange("p (h q) -> p h q", h=8),
                           tab_ap(tab_c, 8 * g * TABL + OFF + 128 * dd - 127, 1, 128), b_dsel[dd], reads=[b_tab], writes=[b_dsel[dd]])
                for dd in range(5):
                    kb.dma('sp', dwin[dd][:].rearrange("p (h q) -> p h q", h=8),
                           tab_ap(tab_w, 8 * g * TABL + OFF + 128 * dd - 127, 1, 128), b_dwin[dd], reads=[b_tab], writes=[b_dwin[dd]])
                for qi in range(16):
                    qs = it % 2; it += 1
                    rows = slice(qi * 128, (qi + 1) * 128)
                    cur['rows'] = rows; cur['g'] = g
                    kb.dma('sp', qTs[qs][:], qT_d[qi, :, g * 1024:(g + 1) * 1024], b_qTs[qs], reads=[qT_b[qi]], writes=[b_qTs[qs]])
                    for br in range(3):
                        c0 = 7168 + br * 4096 + g * 1024
                        kb.dma('sp', z3[:, br, :], proj0[rows, c0:c0 + 1024], b_z3, reads=[proj0_b[qi]], writes=[b_z3])
                    kb.dma('sp', gl[:], proj0[rows, 19456:19552], b_gl, reads=[proj0_b[qi]], writes=[b_gl])
                    kb.dma('sp', Tc[0:127, :].rearrange("p (h q) -> p h q", h=8),
                           tab_ap(tab_c, 8 * g * TABL + OFF + 128 * qi - 2047, 16, 127), b_Tc, reads=[b_tab], writes=[b_Tc])
                    kb.op('act', lambda e: e.activation(out=sig[:], in_=gl[:], func=AF.Sigmoid), reads=[b_gl], writes=[b_sig])
                    kb.op('act', lambda e: e.activation(out=sz[:], in_=z3[:], func=AF.Silu), reads=[b_z3], writes=[b_sz])
                    pcur = ptc % 3; ptc += 1
                    for half in range(2):
                        hs = slice(half * 512, (half + 1) * 512)
                        kb.op('pe', lambda e, half=half, hs=hs, g=g, qs=qs: e.matmul(psum[half][0:127, :], lhsT=kcmpT[:, g, 0:127], rhs=qTs[qs][:, hs],
                                                                                  start=True, stop=False),
                              reads=[b_kcmpT, b_qTs[qs]], writes=[b_ps[half]], inc=False)
                        kb.op('pe', lambda e, half=half, hs=hs: e.matmul(psum[half][0:127, :], lhsT=identb[0:127, 0:127], rhs=Tc[0:127, hs],
                                                                        start=False, stop=True),
                              reads=[b_identb, b_Tc], writes=[b_ps[half]])
                        kb.op('act', lambda e, half=half, hs=hs, pcur=pcur: e.activation(out=pt[pcur][0:127, hs], in_=psum[half][0:127, :], func=AF.Exp),
                              reads=[b_ps[half]], writes=[b_pt[pcur]])
                    for h in range(8):
                        kb.op('pe', lambda e, h=h, pcur=pcur: e.matmul(psum[7][:, h * 33:(h + 1) * 33], lhsT=pt[pcur][0:127, h * 128:(h + 1) * 128],
                                                                      rhs=MT1s[0:127, :], start=True, stop=True),
                              reads=[b_pt[pcur], b_MT1s], writes=[b_ps[7]], inc=(h == 7))
                    for h in range(8):
                        kb.op('pe', lambda e, h=h, pcur=pcur, g=g: e.matmul(psum[4 + h // 4][:, (h % 4) * 128:(h % 4 + 1) * 128],
                                                                           lhsT=pt[pcur][0:127, h * 128:(h + 1) * 128], rhs=vcmp[0:127, g, 0:128],
                                                                           start=True, stop=True),
                              reads=[b_pt[pcur], b_vcmp], writes=[b_ps[4 + h // 4]], inc=(h % 4 == 3))
                    v7 = psum[7][:, 0:264].rearrange("p (h c) -> p h c", c=33)
                    kb.op('dve', lambda e, v7=v7: e.tensor_scalar(out=rc[:].unsqueeze(2), in0=v7[:, :, 32:33], scalar1=1e-30, scalar2=None, op0=ALU.max), reads=[b_ps[7]], writes=[b_sm])
                    kb.op('dve', lambda e: e.reciprocal(out=rc[:], in_=rc[:]), reads=[b_sm], writes=[b_sm])
                    kb.op('dve', lambda e, v7=v7: e.tensor_tensor(out=tmp8[:], in0=v7[:, :, 0:32], in1=rc[:].unsqueeze(2).to_broadcast([128, 8, 32]), op=ALU.mult),
                          reads=[b_ps[7], b_sm], writes=[b_sm])
                    kb.op('dve', lambda e: e.tensor_reduce(out=imps[:], in_=tmp8[:].rearrange("p h j -> p j h"), axis=AX.X, op=ALU.add),
                          reads=[b_sm], writes=[b_sm])
                    kb.op('dve', lambda e, qi=qi: e.tensor_tensor(out=sc1[:], in0=imps[:], in1=vnfs[:, qi, :], op=ALU.mult), reads=[b_sm, b_cst3], writes=[b_sm])
                    kb.op('dve', lambda e, qi=qi: e.tensor_tensor(out=sc1[:], in0=sc1[:], in1=vnfm1s[:, qi, :], op=ALU.add), reads=[b_sm, b_cst3], writes=[b_sm])
                    kb.op('dve', lambda e: e.max(out=mx[:], in_=sc1[:]), reads=[b_sm], writes=[b_sm])
                    kb.op('dve', lambda e: e.match_replace(out=sc2[:], in_to_replace=mx[:], in_values=sc1[:], imm_value=-2.0), reads=[b_sm], writes=[b_sm])
                    kb.op('dve', lambda e: e.max(out=mx2[:], in_=sc2[:]), reads=[b_sm], writes=[b_sm])
                    kb.op('dve', lambda e: e.memset(mx2[:, 5:8], -2.0), reads=[b_sm], writes=[b_sm])
                    kb.op('dve', lambda e: e.match_replace(out=sc3[:], in_to_replace=mx2[:], in_values=sc2[:], imm_value=-2.0), reads=[b_sm], writes=[b_sm])
                    kb.op('dve', lambda e: e.tensor_tensor(out=selm[:], in0=sc1[:], in1=sc3[:], op=ALU.subtract), reads=[b_sm], writes=[b_sm])
                    kb.op('dve', lambda e: e.tensor_scalar(out=selm[:], in0=selm[:], scalar1=1.0, scalar2=None, op0=ALU.min), reads=[b_sm], writes=[b_sm])
                    kb.op('dve', lambda e, qi=qi: e.tensor_tensor(out=selm[:], in0=selm[:], in1=forceds[:, qi, :], op=ALU.max), reads=[b_sm, b_cst3], writes=[b_sm])
                    kb.op('dve', lambda e: e.tensor_scalar(out=selm[:], in0=selm[:], scalar1=-1.0, scalar2=-NEGM, op0=ALU.add, op1=ALU.mult),
                          reads=[b_sm], writes=[b_sm])
                    if DEBUG:
                        kb.dma('sp', dbg_selm[rows, g, :], selm[:], b_sm, reads=[b_sm], writes=[b_dbg])
                        kb.dma('sp', dbg_imps[rows, g, :], imps[:], b_sm, reads=[b_sm], writes=[b_dbg])
                    kb.op('dve', lambda e, g=g: e.tensor_tensor(out=cc[:], in0=rc[:], in1=sig[:, 8 * g:8 * g + 8], op=ALU.mult), reads=[b_sm, b_sig], writes=[b_sm])
                    for bi in range(2):
                        view = psum[4 + bi][:, 0:512].rearrange("p (h c) -> p h c", c=128)
                        kb.op('dve', lambda e, bi=bi, view=view: e.tensor_tensor(
                            out=tmpo[:, 4 * bi:4 * bi + 4, :], in0=view, in1=cc[:, 4 * bi:4 * bi + 4].unsqueeze(2).to_broadcast([128, 4, 128]), op=ALU.mult),
                            reads=[b_ps[4 + bi], b_sm], writes=[b_tmpo])
                    fin_mix(0, True)
                    kb.op('pe', lambda e: e.transpose(psum[7][0:32, 0:128], selm[:], ident[:]), reads=[b_sm, b_ident], writes=[b_ps[7]])
                    kb.op('dve', lambda e: e.tensor_copy(out=mbT4[:], in_=psum[7][0:32, 0:128].unsqueeze(1).to_broadcast([32, 4, 128])),
                          reads=[b_ps[7]], writes=[b_mbT4])
                    for kt in range(qi + 1):
                        rt = 15 - kt
                        pcur = ptc % 3; ptc += 1
                        for half in range(2):
                            b = 2 * (kt % 2) + half
                            hs = slice(half * 512, (half + 1) * 512)
                            kb.op('pe', lambda e, b=b, hs=hs, g=g, rt=rt, qs=qs: e.matmul(psum[b][:, :], lhsT=ksT[:, g, rt * 128:(rt + 1) * 128], rhs=qTs[qs][:, hs],
                                                                                       start=True, stop=False),
                                  reads=[b_ksT, b_qTs[qs]], writes=[b_ps[b]], inc=False)
                            kb.op('pe', lambda e, b=b, hs=hs, dd=qi - kt: e.matmul(psum[b][:, :], lhsT=identb[:], rhs=dsel[dd][:, hs], start=False, stop=False),
                                  reads=[b_identb, b_dsel[qi - kt]], writes=[b_ps[b]], inc=False)
                            kb.op('pe', lambda e, b=b, kt=kt: e.matmul(psum[b][:, :], lhsT=Es[:, kt, :], rhs=mbT4[:].rearrange("j h q -> j (h q)"),
                                                                      start=False, stop=True),
                                  reads=[b_Es, b_mbT4], writes=[b_ps[b]])
                            kb.op('act', lambda e, b=b, hs=hs, pcur=pcur: e.activation(out=pt[pcur][:, hs], in_=psum[b][:, :], func=AF.Exp),
                                  reads=[b_ps[b]], writes=[b_pt[pcur]])
                        for h in range(8):
                            kb.op('pe', lambda e, h=h, pcur=pcur, rt=rt, g=g, kt=kt: e.matmul(
                                psum[4 + h // 3][:, (h % 3) * 129:(h % 3) * 129 + 129], lhsT=pt[pcur][:, h * 128:(h + 1) * 128], rhs=vsr[:, rt, g, :],
                                start=(kt == 0 and h % 3 == 0), stop=(kt == qi)),
                                reads=[b_pt[pcur], b_vsr], writes=[b_ps[4 + h // 3]], inc=(h in (2, 5, 7)))
                    branch_epilogue(1, g, False)
                    kts = list(range(max(0, qi - 4), qi + 1))
                    for kt in kts:
                        rt = 15 - kt
                        pcur = ptc % 3; ptc += 1
                        for half in range(2):
                            b = 2 * (kt % 2) + half
                            hs = slice(half * 512, (half + 1) * 512)
                            kb.op('pe', lambda e, b=b, hs=hs, g=g, rt=rt, qs=qs: e.matmul(psum[b][:, :], lhsT=kwT[:, g, rt * 128:(rt + 1) * 128], rhs=qTs[qs][:, hs],
                                                                                       start=True, stop=False),
                                  reads=[b_kwT, b_qTs[qs]], writes=[b_ps[b]], inc=False)
                            kb.op('pe', lambda e, b=b, hs=hs, dd=qi - kt: e.matmul(psum[b][:, :], lhsT=identb[:], rhs=dwin[dd][:, hs], start=False, stop=True),
                                  reads=[b_identb, b_dwin[qi - kt]], writes=[b_ps[b]])
                            kb.op('act', lambda e, b=b, hs=hs, pcur=pcur: e.activation(out=pt[pcur][:, hs], in_=psum[b][:, :], func=AF.Exp),
                                  reads=[b_ps[b]], writes=[b_pt[pcur]])
                        for h in range(8):
                            kb.op('pe', lambda e, h=h, pcur=pcur, rt=rt, g=g, kt=kt: e.matmul(
                                psum[4 + h // 3][:, (h % 3) * 129:(h % 3) * 129 + 129], lhsT=pt[pcur][:, h * 128:(h + 1) * 128], rhs=vwr[:, rt, g, :],
                                start=(kt == kts[0] and h % 3 == 0), stop=(kt == qi)),
                                reads=[b_pt[pcur], b_vwr], writes=[b_ps[4 + h // 3]], inc=(h in (2, 5, 7)))
                    branch_epilogue(2, g, False)
                    kb.dma('sp', mixed_d[rows, g * 1024:(g + 1) * 1024], mixed[:], b_mixed, reads=[b_mixed], writes=[mixed_b[qi]])
            kb.barrier()
    if RUN_S:
        page_tab = dt_in("page_tab", [NS, 128], I32)
        cache_kv = dt_in("cache_kv", [2 * 1280 * 128, 1024])
        cache_win = dt_in("cache_win", [NS * 512, 1024])
        iota_p = dt_in("iota_p", [128, 1])
        MTs_d = dt_in("MTs", [128, 8, 257], BF16)
        E2_d = dt_in("E2", [2, 128], BF16)
        vnf_s_d = dt_in("vnf_s", [32, 2, 257]); forced_s_d = dt_in("forced_s", [32, 257])
        mb_d = nc.dram_tensor("mb_d", [NS, 258, 256], BF16); b_mbd = [Buf(f"mbd{i}") for i in range(NS)]

        def tab_ap_s(tab, off, pstep):
            return bass.AP(tab, off, [[pstep, 128], [TABL, 32], [1, 8]])

        with ExitStack() as esS:
            SBs = lambda name, shape, dt=F32: esS.enter_context(nc.sbuf_tensor(name, list(shape), dt))
            identb2 = SBs("identb2", [128, 128], BF16); b_identb2 = Buf("identb2")
            kb.op('dve', lambda e: e.tensor_copy(out=identb2[:], in_=ident[:]), reads=[b_ident], writes=[b_identb2])
            onesb = SBs("onesb", [128, 1], BF16); ones_r = SBs("ones_r", [1, 128]); b_onesS = Buf("onesS")
            kb.op('dve', lambda e: e.memset(onesb[:], 1.0), writes=[b_onesS])
            kb.op('dve', lambda e: e.memset(ones_r[:], 1.0), writes=[b_onesS])
            MTs = SBs("MTs_sb", [128, 8, 257], BF16); E2s = SBs("E2s", [2, 128], BF16); b_cS = Buf("constS")
            vnf_s = SBs("vnf_s_sb", [32, 2, 257]); forced_s = SBs("forced_s_sb", [32, 257]); iot = SBs("iot", [128, 1])
            kb.dma('sp', MTs[:], MTs_d.ap(), b_cS, writes=[b_cS]); kb.dma('sp', E2s[:], E2_d.ap(), b_cS, writes=[b_cS])
            kb.dma('sp', vnf_s[:], vnf_s_d.ap(), b_cS, writes=[b_cS]); kb.dma('sp', forced_s[:], forced_s_d.ap(), b_cS, writes=[b_cS])
            kb.dma('sp', iot[:], iota_p.ap(), b_cS, writes=[b_cS])
            TcF = SBs("TcF", [128, 256], BF16); Tc7 = SBs("Tc7", [128, 256], BF16); TsF = SBs("TsF", [128, 256], BF16)
            Tsn = [SBs(f"Tsn{i}", [128, 256], BF16) for i in range(9)]; Twn = [SBs(f"Twn{i}", [128, 256], BF16) for i in range(5)]
            b_T = Buf("Tsample")
            v3 = lambda tt: tt[:].rearrange("p (h q) -> p h q", q=8)
            kb.dma('sp', v3(TcF), tab_ap_s(tab_c, OFF + 14321, 16), b_T, reads=[b_tab], writes=[b_T])
            kb.dma('sp', v3(Tc7), tab_ap_s(tab_c, OFF + 14321 - 2048 * 7, 16), b_T, reads=[b_tab], writes=[b_T])
            kb.dma('sp', v3(TsF), tab_ap_s(tab_c, OFF + 16384 - 127, 1), b_T, reads=[b_tab], writes=[b_T])
            for i in range(9):
                kb.dma('sp', v3(Tsn[i]), tab_ap_s(tab_c, OFF + 16384 - 128 * (120 + i) - 127, 1), b_T, reads=[b_tab], writes=[b_T])
            for w_ in range(5):
                kb.dma('sp', v3(Twn[w_]), tab_ap_s(tab_w, OFF + 512 - 128 * w_ - 127, 1), b_T, reads=[b_tab], writes=[b_T])

            pg = [SBs(f"pg{i}", [128, 1024]) for i in range(2)]; b_pg = [Buf("pg0"), Buf("pg1")]
            kcmpT_s = SBs("kcmpT_s", [128, 4, 1024], BF16); b_kcs = Buf("kcmpT_s")
            vcmp_s = SBs("vcmp_s", [128, 8, 4, 129], BF16); b_vcs = Buf("vcmp_s")
            ptb_i = SBs("ptb_i", [128, 128], I32); ptb_f = SBs("ptb_f", [128, 128]); idx_i = SBs("idx_i", [128, 128], I32); b_idx = Buf("idx")
            qs16 = SBs("qs16", [128, 32, 128], BF16); b_qs = Buf("qs16")
            kpg = [SBs(f"kpg{i}", [128, 4, 128], BF16) for i in range(2)]; b_kpg = [Buf("kpg0"), Buf("kpg1")]
            vpg = [SBs(f"vpg{i}", [128, 4, 129], BF16) for i in range(2)]; b_vpg = [Buf("vpg0"), Buf("vpg1")]
            Ps = [SBs(f"Ps{i}", [128, 256], BF16) for i in range(2)]; b_Ps = [Buf("Ps0"), Buf("Ps1")]
            kb.op('dve', lambda e: e.memset(vcmp_s[:, :, :, 128:129], 1.0), writes=[b_vcs])
            for i in range(2):
                kb.op('dve', lambda e, i=i: e.memset(vpg[i][:, :, 128:129], 1.0), writes=[b_vpg[i]])

            def compress_tile(bt, b):
                for kv in range(2):
                    for g in range(4):
                        pi = 4 + (kv * 4 + g) % 2
                        for r in range(32):
                            kb.op('pe', lambda e, kv=kv, g=g, r=r, pi=pi: e.matmul(psum[pi][:, 0:128], lhsT=w1s[:, kv, r, :], rhs=xT[b][:, kv * 4 + g, r:r + 2033:16],
                                                                                  start=(r == 0), stop=(r == 31)),
                                  reads=[b_w1s, b_xT[b]], writes=[b_ps[pi]], inc=(r == 31))
                        kb.op('act', lambda e, kv=kv, pi=pi: e.activation(out=hid[:], in_=psum[pi][:, 0:128], func=AF.Silu, bias=pec[:, kv:kv + 1]),
                              reads=[b_ps[pi], b_pec], writes=[b_hid])
                        po = 6 + (kv * 4 + g) % 2
                        kb.op('pe', lambda e, kv=kv, po=po: e.matmul(psum[po][:, 0:128], lhsT=hid[:], rhs=w2s[:, kv, :], start=True, stop=True),
                              reads=[b_hid, b_w2s], writes=[b_ps[po]])
                        if kv == 0:
                            kb.op('act', lambda e, po=po: e.activation(out=cjunk[:], in_=psum[po][:, 0:128], func=AF.Square, accum_out=cst[:, 0:1]),
                                  reads=[b_ps[po]], writes=[b_cj, b_cst])
                            kb.op('dve', lambda e: e.tensor_scalar(out=cst[:, 1:2], in0=cst[:, 0:1], scalar1=1.0 / 128, scalar2=EPS, op0=ALU.mult, op1=ALU.add), reads=[b_cst], writes=[b_cst])
                            kb.op('act', lambda e: e.activation(out=cst[:, 3:4], in_=cst[:, 1:2], func=AF.Sqrt), reads=[b_cst], writes=[b_cst])
                            kb.op('dve', lambda e: e.reciprocal(out=cst[:, 2:3], in_=cst[:, 3:4]), reads=[b_cst], writes=[b_cst])
                            kb.op('dve', lambda e, po=po: e.scalar_tensor_tensor(out=kcn[:], in0=psum[po][:, 0:128], scalar=cst[:, 2:3], in1=qkg[:, 1, :], op0=ALU.mult, op1=ALU.mult),
                                  reads=[b_ps[po], b_cst, b_qkg], writes=[b_kcn])
                            kb.op('pe', lambda e, po=po: e.transpose(psum[po][:, 128:256], kcn[:], jmat[:]), reads=[b_kcn, b_jmat], writes=[b_ps[po]])
                            kb.op('dve', lambda e, po=po, g=g: e.tensor_copy(out=kcmpT_s[:, g, bt * 128:(bt + 1) * 128], in_=psum[po][:, 128:256]),
                                  reads=[b_ps[po]], writes=[b_kcs])
                        else:
                            kb.op('dve', lambda e, po=po: e.tensor_copy(out=kcn[:], in_=psum[po][:, 0:128]), reads=[b_ps[po]], writes=[b_kcn])
                            kb.op('pe', lambda e, po=po: e.matmul(psum[po][:, 128:256], lhsT=jmat[:], rhs=kcn[:], start=True, stop=True), reads=[b_kcn, b_jmat], writes=[b_ps[po]])
                            kb.op('dve', lambda e, po=po, g=g: e.tensor_copy(out=vcmp_s[:, bt, g, 0:128], in_=psum[po][:, 128:256]), reads=[b_ps[po]], writes=[b_vcs])

            def attend_tile(kT, b_kT, vT, b_vT, bias, mask_p, first, last, ti):
                b = ti % 2
                for g in range(4):
                    kb.op('pe', lambda e, g=g, b=b: e.matmul(psum[b][:, g * 64:(g + 1) * 64].rearrange("p (h q) -> p h q", q=8), lhsT=kT(g), rhs=qs16[:, 8 * g:8 * g + 8, 0:8],
                                                              start=(g == 0), stop=False), reads=[b_kT, b_qs], writes=[b_ps[b]], inc=False)
                kb.op('pe', lambda e, b=b: e.matmul(psum[b][:, 0:256], lhsT=identb2[:], rhs=bias[:], start=False, stop=(mask_p is None)),
                      reads=[b_identb2, b_T], writes=[b_ps[b]], inc=(mask_p is None))
                if mask_p is not None:
                    kb.op('pe', lambda e, b=b: e.matmul(psum[b][:, 0:256], lhsT=E2s[:], rhs=mb2[:, mask_p, :], start=False, stop=True),
                          reads=[b_cS, b_mb2], writes=[b_ps[b]])
                kb.op('act', lambda e, b=b: e.activation(out=Ps[b][:], in_=psum[b][:, 0:256], func=AF.Exp), reads=[b_ps[b]], writes=[b_Ps[b]])
                for g in range(4):
                    kb.op('pe', lambda e, g=g, b=b: e.matmul(psum[2 + g // 2][0:64, (g % 2) * 129:(g % 2) * 129 + 129], lhsT=Ps[b][:, g * 64:(g + 1) * 64], rhs=vT(g),
                                                              start=(first and g % 2 == 0), stop=last), reads=[b_Ps[b], b_vT], writes=[b_ps[2 + g // 2]], inc=(g % 2 == 1))

            def finish_branch(br):
                for bi in range(2):
                    view = psum[2 + bi][0:64, 0:258].rearrange("p (g c) -> p g c", c=129)
                    kb.op('dve', lambda e, bi=bi, view=view: e.tensor_scalar(out=rr[:, 2 * bi:2 * bi + 2].unsqueeze(2), in0=view[:, :, 128:129], scalar1=1e-30, scalar2=None, op0=ALU.max),
                          reads=[b_ps[2 + bi]], writes=[b_rr])
                kb.op('dve', lambda e: e.reciprocal(out=rr[:], in_=rr[:]), reads=[b_rr], writes=[b_rr])
                for bi in range(2):
                    view = psum[2 + bi][0:64, 0:258].rearrange("p (g c) -> p g c", c=129)
                    kb.op('dve', lambda e, bi=bi, view=view: e.tensor_tensor(out=obr[:, br, 2 * bi:2 * bi + 2, :], in0=view[:, :, 0:128],
                                                                            in1=rr[:, 2 * bi:2 * bi + 2].unsqueeze(2).to_broadcast([64, 2, 128]), op=ALU.mult),
                          reads=[b_ps[2 + bi], b_rr], writes=[b_obr])

            for si in range(NS):
                ts_ = 16 + si
                es_p1 = ExitStack()
                SB1 = lambda name, shape, dt=F32: es_p1.enter_context(nc.sbuf_tensor(f"{name}{si}", list(shape), dt))
                w1s = SB1("w1s_s", [128, 2, 32, 128], BF16); w2s = SB1("w2s_s", [128, 2, 128], BF16); b_w1s = Buf("w1s_s"); b_w2s = Buf("w2s_s")
                peTs = SB1("peTs_s", [128, 2, 32]); peTb = SB1("peTb_s", [128, 2, 32], BF16); pec = SB1("pec_s", [128, 2]); b_pe = Buf("pe_s"); b_pec = Buf("pec_s")
                for kv in range(2):
                    kb.dma('pool', w1s[:, kv, :, :], cmp_w1[kv].rearrange("(r d) j -> d r j", d=128), b_w1s, writes=[b_w1s])
                    kb.dma('pool', w2s[:, kv, :], cmp_w2[kv], b_w2s, writes=[b_w2s])
                kb.dma('sp', peTs[:], peT.ap(), b_pe, writes=[b_pe])
                kb.op('dve', lambda e: e.tensor_copy(out=peTb[:], in_=peTs[:]), reads=[b_pe], writes=[b_pe])
                for kv in range(2):
                    for r in range(32):
                        kb.op('pe', lambda e, kv=kv, r=r: e.matmul(psum[0][:, kv:kv + 1], lhsT=w1s[:, kv, r, :], rhs=peTb[:, kv, r:r + 1], start=(r == 0), stop=(r == 31)),
                              reads=[b_w1s, b_pe], writes=[b_ps[0]], inc=(r == 31))
                kb.op('dve', lambda e: e.tensor_copy(out=pec[:], in_=psum[0][:, 0:2]), reads=[b_ps[0]], writes=[b_pec])
                xT = [SB1(f"xTs{i}", [128, 8, 2064], BF16) for i in range(2)]; b_xT = [Buf("xTs0"), Buf("xTs1")]
                hid = SB1("hid_s", [128, 128], BF16); b_hid = Buf("hid_s")
                kcn = SB1("kcn_s", [128, 128]); b_kcn = Buf("kcn_s"); cjunk = SB1("cjunk_s", [128, 128]); b_cj = Buf("cj_s")
                cst = SB1("cst_s", [128, 4]); b_cst = Buf("cst_s")
                kb.dma('sp', ptb_i[:], bass.AP(page_tab, si * 128, [[0, 128], [1, 128]]), b_idx, writes=[b_idx])
                kb.op('dve', lambda e: e.tensor_copy(out=ptb_f[:], in_=ptb_i[:]), reads=[b_idx], writes=[b_idx])
                kb.op('dve', lambda e: e.tensor_scalar(out=ptb_f[:], in0=ptb_f[:], scalar1=256.0, scalar2=iot[:, 0:1], op0=ALU.mult, op1=ALU.add), reads=[b_idx, b_cS], writes=[b_idx])
                kb.op('dve', lambda e: e.tensor_copy(out=idx_i[:], in_=ptb_f[:]), reads=[b_idx], writes=[b_idx])
                kb.dma('sp', qs16[:].rearrange("p h q -> p (h q)"), qT_d[ts_], b_qs, reads=[qT_b[ts_]], writes=[b_qs])
                if SCUT == 0:
                    kb.finish()
                    es_p1.close()
                    return nc
                for p in range(128):
                    s = p % 2; bt = p // 16; b = bt % 2; pl = p % 16
                    kb.dma('pool', pg[s][:], cache_kv[:, :], b_pg[s], reads=[b_idx], writes=[b_pg[s]], indirect_idx=idx_i[:, p:p + 1])
                    for kv in range(2):
                        pi = 4 + kv
                        for g in range(4):
                            kb.op('pe', lambda e, kv=kv, g=g, pi=pi, s=s: e.transpose(psum[pi][:, g * 128:(g + 1) * 128], pg[s][:, kv * 512 + g * 128:kv * 512 + (g + 1) * 128], ident[:]),
                                  reads=[b_pg[s], b_ident], writes=[b_ps[pi]], inc=(g == 3))
                        sv = psum[pi][:, 0:512].rearrange("p (g k) -> p g k", g=4)
                        if kv == 0:
                            kb.op('act', lambda e, sv=sv, b=b, pl=pl: e.copy(out=xT[b][:, 0:4, pl * 128:(pl + 1) * 128], in_=sv), reads=[b_ps[pi]], writes=[b_xT[b]])
                        else:
                            kb.op('dve', lambda e, sv=sv, b=b, pl=pl: e.tensor_copy(out=xT[b][:, 4:8, pl * 128:(pl + 1) * 128], in_=sv), reads=[b_ps[pi]], writes=[b_xT[b]])
                        if pl == 0 and p > 0:
                            if kv == 0:
                                kb.op('act', lambda e, sv=sv, b=b, kv=kv: e.copy(out=xT[1 - b][:, kv * 4:kv * 4 + 4, 2048:2064], in_=sv[:, :, 0:16]), reads=[b_ps[pi]], writes=[b_xT[1 - b]])
                            else:
                                kb.op('dve', lambda e, sv=sv, b=b, kv=kv: e.tensor_copy(out=xT[1 - b][:, kv * 4:kv * 4 + 4, 2048:2064], in_=sv[:, :, 0:16]), reads=[b_ps[pi]], writes=[b_xT[1 - b]])
                    if pl == 0 and p > 0:
                        compress_tile(bt - 1, 1 - b)
                kb.op('dve', lambda e: e.memset(xT[1][:, :, 2048:2064], 0.0), writes=[b_xT[1]])
                compress_tile(7, 1)
                kb.barrier()
                if DEBUG and si == 0:
                    b_dbgS = Buf('dbgS')
                    dk_ = nc.dram_tensor('dbg_kcs', [128, 4, 1024], BF16, kind='ExternalOutput')
                    dv_ = nc.dram_tensor('dbg_vcs', [128, 8, 4, 129], BF16, kind='ExternalOutput')
                    kb.dma('sp', dk_.ap(), kcmpT_s[:], b_dbgS, reads=[b_kcs], writes=[b_dbgS])
                    kb.dma('sp', dv_.ap(), vcmp_s[:], b_dbgS, reads=[b_vcs], writes=[b_dbgS])
                    kb.barrier()
                if SCUT == 1:
                    kb.finish()
                    es_p1.close()
                    return nc
                es_p1.close()
                es_p2 = ExitStack()
                SB2 = lambda name, shape, dt=F32: es_p2.enter_context(nc.sbuf_tensor(f"{name}{si}", list(shape), dt))
                PcT = SB2("PcT", [128, 8, 256], BF16); b_PcT = Buf("PcT")
                rsr = SB2("rsr", [1, 256]); rbc = SB2("rbc", [128, 256]); b_rs = Buf("rsr"); b_rbc = Buf("rbc")
                pn = SB2("pn", [128, 256]); impT = SB2("impT", [128, 32]); impTb = SB2("impTb", [128, 32], BF16); b_pn = Buf("pn")
                sS = {n: SB2(n + "_s", [32, 257]) for n in ("sc1", "sc2", "sc3", "selm")}; mxs = SB2("mx_s", [32, 8]); mx2s = SB2("mx2_s", [32, 8]); b_sS = Buf("smallS")
                mbx = SB2("mbx", [128, 4, 8, 8], BF16); b_mbx = Buf("mbx"); zrow = SB2("zrow", [1, 256], BF16)
                mb2 = SB2("mb2", [2, 43, 256], BF16); b_mb2 = Buf("mb2")
                obr = SB2("obr", [64, 3, 4, 128]); b_obr = Buf("obr"); rr = SB2("rr_s", [64, 4]); b_rr = Buf("rr_s")
                zs = SB2("zs", [64, 3, 4, 128]); gls = SB2("gls", [64, 3, 4]); b_zs = Buf("zs"); b_gls = Buf("gls")
                mixs = SB2("mixs", [64, 4, 128]); b_mixs = Buf("mixs"); tmps = SB2("tmps", [64, 4, 128])

                kb.op('dve', lambda e: e.memset(zrow[:], 0.0), writes=[b_mbx])
                for bt in range(8):
                    attend_tile(lambda g, bt=bt: kcmpT_s[:, g, bt * 128:(bt + 1) * 128], b_kcs, lambda g, bt=bt: vcmp_s[:, bt, g, :], b_vcs,
                                Tc7 if bt == 7 else TcF, None, bt == 0, bt == 7, bt)
                    kb.op('act', lambda e, bt=bt: e.copy(out=PcT[:, bt, :], in_=Ps[bt % 2][:]), reads=[b_Ps[bt % 2]], writes=[b_PcT])
                    kb.op('pe', lambda e, bt=bt: e.matmul(psum[6][0:1, 0:256], lhsT=onesb[:, 0:1], rhs=Ps[bt % 2][:], start=(bt == 0), stop=(bt == 7)),
                          reads=[b_onesS, b_Ps[bt % 2]], writes=[b_ps[6]], inc=(bt == 7))
                finish_branch(0)
                if SCUT == 2:
                    kb.finish()
                    es_p2.close()
                    return nc
                kb.op('dve', lambda e: e.tensor_scalar(out=rsr[:], in0=psum[6][0:1, 0:256], scalar1=1e-30, scalar2=None, op0=ALU.max), reads=[b_ps[6]], writes=[b_rs])
                kb.op('dve', lambda e: e.reciprocal(out=rsr[:], in_=rsr[:]), reads=[b_rs], writes=[b_rs])
                kb.op('pe', lambda e: e.matmul(psum[7][:, 0:256], lhsT=ones_r[:], rhs=rsr[:], start=True, stop=True), reads=[b_onesS, b_rs], writes=[b_ps[7]])
                kb.op('act', lambda e: e.copy(out=rbc[:], in_=psum[7][:, 0:256]), reads=[b_ps[7]], writes=[b_rbc])
                for bt in range(8):
                    kb.op('dve', lambda e, bt=bt: e.tensor_tensor(out=pn[:], in0=PcT[:, bt, :], in1=rbc[:], op=ALU.mult), reads=[b_PcT, b_rbc], writes=[b_pn])
                    kb.op('dve', lambda e: e.tensor_reduce(out=impT[:].rearrange("p (g q) -> p g q", g=4), in_=pn[:].rearrange("p (g h q) -> p g q h", g=4, h=8), axis=AX.X, op=ALU.add),
                          reads=[b_pn], writes=[b_pn])
                    kb.op('dve', lambda e: e.tensor_copy(out=impTb[:], in_=impT[:]), reads=[b_pn], writes=[b_pn])
                    kb.op('pe', lambda e, bt=bt: e.matmul(psum[6][0:32, 0:257], lhsT=impTb[:], rhs=MTs[:, bt, :], start=(bt == 0), stop=(bt == 7)),
                          reads=[b_pn, b_cS], writes=[b_ps[6]])
                sc1, sc2, sc3, selm = sS["sc1"], sS["sc2"], sS["sc3"], sS["selm"]
                kb.op('dve', lambda e: e.tensor_tensor(out=sc1[:], in0=psum[6][0:32, 0:257], in1=vnf_s[:, 0, :], op=ALU.mult), reads=[b_ps[6], b_cS], writes=[b_sS])
                kb.op('dve', lambda e: e.tensor_tensor(out=sc1[:], in0=sc1[:], in1=vnf_s[:, 1, :], op=ALU.add), reads=[b_sS, b_cS], writes=[b_sS])
                kb.op('dve', lambda e: e.max(out=mxs[:], in_=sc1[:]), reads=[b_sS], writes=[b_sS])
                kb.op('dve', lambda e: e.match_replace(out=sc2[:], in_to_replace=mxs[:], in_values=sc1[:], imm_value=-2.0), reads=[b_sS], writes=[b_sS])
                kb.op('dve', lambda e: e.max(out=mx2s[:], in_=sc2[:]), reads=[b_sS], writes=[b_sS])
                kb.op('dve', lambda e: e.memset(mx2s[:, 5:8], -2.0), reads=[b_sS], writes=[b_sS])
                kb.op('dve', lambda e: e.match_replace(out=sc3[:], in_to_replace=mx2s[:], in_values=sc2[:], imm_value=-2.0), reads=[b_sS], writes=[b_sS])
                kb.op('dve', lambda e: e.tensor_tensor(out=selm[:], in0=sc1[:], in1=sc3[:], op=ALU.subtract), reads=[b_sS], writes=[b_sS])
                kb.op('dve', lambda e: e.tensor_scalar(out=selm[:], in0=selm[:], scalar1=1.0, scalar2=None, op0=ALU.min), reads=[b_sS], writes=[b_sS])
                kb.op('dve', lambda e: e.tensor_tensor(out=selm[:], in0=selm[:], in1=forced_s[:], op=ALU.max), reads=[b_sS, b_cS], writes=[b_sS])
                kb.op('dve', lambda e: e.tensor_scalar(out=selm[:], in0=selm[:], scalar1=-1.0, scalar2=-NEGM, op0=ALU.add, op1=ALU.mult), reads=[b_sS], writes=[b_sS])
                for j0, nj in ((0, 128), (128, 128), (256, 1)):
                    kb.op('pe', lambda e, j0=j0, nj=nj: e.transpose(psum[7][0:nj, 0:32], selm[:, j0:j0 + nj], ident[0:32, 0:32]), reads=[b_sS, b_ident], writes=[b_ps[7]])
                    kb.op('dve', lambda e, nj=nj: e.tensor_copy(out=mbx[0:nj, :, :, :], in_=psum[7][0:nj, 0:32].rearrange("p (g q) -> p g q", g=4).unsqueeze(2).to_broadcast([nj, 4, 8, 8])),
                          reads=[b_ps[7]], writes=[b_mbx])
                    kb.dma('sp', mb_d[si, j0:j0 + nj, :], mbx[0:nj, :, :, :].rearrange("p g h q -> p (g h q)"), b_mbx, reads=[b_mbx], writes=[b_mbd[si]])
                kb.dma('sp', mb_d[si, 257:258, :], zrow[:], b_mbx, reads=[b_mbx], writes=[b_mbd[si]])
                if SCUT == 3:
                    kb.finish()
                    es_p2.close()
                    return nc
                for p in range(129):
                    s = p % 2
                    if p % 43 == 0:
                        npg = min(43, 129 - p)
                        kb.dma('sp', mb2[:, 0:npg, :], mb_d[si, 2 * p:2 * p + 2 * npg, :].rearrange("(p two) c -> two p c", two=2), b_mb2, reads=[b_mbd[si]], writes=[b_mb2])
                    if p < 128:
                        kb.dma('pool', pg[s][:], cache_kv[:, :], b_pg[s], reads=[b_idx], writes=[b_pg[s]], indirect_idx=idx_i[:, p:p + 1], element_offset=1024)
                        for g in range(4):
                            kb.op('pe', lambda e, g=g, s=s: e.transpose(psum[4][:, g * 128:(g + 1) * 128], pg[s][:, g * 128:(g + 1) * 128], jmat[:]),
                                  reads=[b_pg[s], b_jmat], writes=[b_ps[4]], inc=(g == 3))
                        kb.op('act', lambda e, s=s: e.copy(out=kpg[s][:], in_=psum[4][:, 0:512].rearrange("p (g k) -> p g k", g=4)), reads=[b_ps[4]], writes=[b_kpg[s]])
                        kb.op('pe', lambda e, s=s: e.matmul(psum[5][:, 0:512], lhsT=jmat[:], rhs=pg[s][:, 512:1024], start=True, stop=True), reads=[b_pg[s], b_jmat], writes=[b_ps[5]])
                        kb.op('dve', lambda e, s=s: e.tensor_copy(out=vpg[s][:, :, 0:128], in_=psum[5][:, 0:512].rearrange("p (g k) -> p g k", g=4)), reads=[b_ps[5]], writes=[b_vpg[s]])
                        kT = (lambda g, s=s: kpg[s][:, g, :]); bkT = b_kpg[s]; vT = (lambda g, s=s: vpg[s][:, g, :]); bvT = b_vpg[s]
                    else:
                        kT = (lambda g: ksTn[si][:, g, :]); bkT = b_newkv; vT = (lambda g: vsrn[si][:, g, :]); bvT = b_newkv
                    bias = TsF if p < 120 else Tsn[p - 120]
                    attend_tile(kT, bkT, vT, bvT, bias, p % 43, p == 0, p == 128, p)
                finish_branch(1)
                if SCUT == 4:
                    kb.finish()
                    es_p2.close()
                    return nc
                for w_ in range(5):
                    s = w_ % 2
                    if w_ < 4:
                        kb.dma('sp', pg[s][:], cache_win[si * 512 + w_ * 128:si * 512 + (w_ + 1) * 128, :], b_pg[s], writes=[b_pg[s]])
                        k0 = 8 if w_ == 0 else 0
                        kb.dma('sp', win_s[si * 512 + w_ * 128 - 8 + k0:si * 512 + w_ * 128 + 120, :], pg[s][k0:128, :], b_pg[s], reads=[b_pg[s]], writes=[b_outS])
                        for g in range(4):
                            kb.op('pe', lambda e, g=g, s=s: e.transpose(psum[4][:, g * 128:(g + 1) * 128], pg[s][:, g * 128:(g + 1) * 128], jmat[:]),
                                  reads=[b_pg[s], b_jmat], writes=[b_ps[4]], inc=(g == 3))
                        kb.op('act', lambda e, s=s: e.copy(out=kpg[s][:], in_=psum[4][:, 0:512].rearrange("p (g k) -> p g k", g=4)), reads=[b_ps[4]], writes=[b_kpg[s]])
                        kb.op('pe', lambda e, s=s: e.matmul(psum[5][:, 0:512], lhsT=jmat[:], rhs=pg[s][:, 512:1024], start=True, stop=True), reads=[b_pg[s], b_jmat], writes=[b_ps[5]])
                        kb.op('dve', lambda e, s=s: e.tensor_copy(out=vpg[s][:, :, 0:128], in_=psum[5][:, 0:512].rearrange("p (g k) -> p g k", g=4)), reads=[b_ps[5]], writes=[b_vpg[s]])
                        kT = (lambda g, s=s: kpg[s][:, g, :]); bkT = b_kpg[s]; vT = (lambda g, s=s: vpg[s][:, g, :]); bvT = b_vpg[s]
                    else:
                        kT = (lambda g: kwTn[si][:, g, :]); bkT = b_newkv; vT = (lambda g: vwrn[si][:, g, :]); bvT = b_newkv
                    attend_tile(kT, bkT, vT, bvT, Twn[w_], None, w_ == 0, w_ == 4, w_)
                finish_branch(2)
                if DEBUG and si == 0:
                    do_ = nc.dram_tensor('dbg_obr', [64, 3, 4, 128], F32, kind='ExternalOutput')
                    kb.dma('sp', do_.ap(), obr[:], b_dbgS, reads=[b_obr], writes=[b_dbgS])
                if SCUT == 5:
                    kb.finish()
                    es_p2.close()
                    return nc
                r0 = ts_ * 128
                for h in range(8):
                    kb.dma('sp', zs[h * 8:(h + 1) * 8, :, :, :], bass.AP(proj0, r0 * A_IN + 7168 + h * 128, [[A_IN, 8], [4096, 3], [1024, 4], [1, 128]]),
                           b_zs, reads=[proj0_b[ts_]], writes=[b_zs])
                    kb.dma('sp', gls[h * 8:(h + 1) * 8, :, :], bass.AP(proj0, r0 * A_IN + 19456 + h, [[A_IN, 8], [32, 3], [8, 4]]),
                           b_gls, reads=[proj0_b[ts_]], writes=[b_gls], allow_slow_non_contiguous=True)
                kb.op('act', lambda e: e.activation(out=gls[:], in_=gls[:], func=AF.Sigmoid), reads=[b_gls], writes=[b_gls])
                kb.op('act', lambda e: e.activation(out=zs[:], in_=zs[:], func=AF.Silu), reads=[b_zs], writes=[b_zs])
                for br in range(3):
                    kb.op('dve', lambda e, br=br: e.tensor_tensor(out=tmps[:], in0=obr[:, br, :, :], in1=gls[:, br, :].unsqueeze(2).to_broadcast([64, 4, 128]), op=ALU.mult),
                          reads=[b_obr, b_gls], writes=[b_mixs])
                    if br == 0:
                        kb.op('dve', lambda e, br=br: e.tensor_tensor(out=mixs[:], in0=tmps[:], in1=zs[:, br, :, :], op=ALU.mult), reads=[b_mixs, b_zs], writes=[b_mixs])
                    else:
                        kb.op('dve', lambda e, br=br: e.tensor_tensor(out=tmps[:], in0=tmps[:], in1=zs[:, br, :, :], op=ALU.mult), reads=[b_mixs, b_zs], writes=[b_mixs])
                        kb.op('dve', lambda e: e.tensor_tensor(out=mixs[:], in0=mixs[:], in1=tmps[:], op=ALU.add), reads=[b_mixs], writes=[b_mixs])
                for h in range(8):
                    kb.dma('sp', bass.AP(mixed_d, r0 * D + h * 128, [[D, 8], [1024, 4], [1, 128]]), mixs[h * 8:(h + 1) * 8, :, :], b_mixs, reads=[b_mixs], writes=[mixed_b[ts_]])
                kb.barrier()
                es_p2.close()
            kb.barrier()
    def make_gate_bc(layer, es):
        gbc = [es.enter_context(nc.sbuf_tensor(f"gbc{r}_{layer}", [128, D], F32)) for r in range(1 + NS)]
        b_gbc = [Buf(f"gbc{r}") for r in range(1 + NS)]
        ones = es.enter_context(nc.sbuf_tensor(f"ones_{layer}", [128, 128], F32)); b_ones = Buf("ones")
        dg = [es.enter_context(nc.sbuf_tensor(f"dg{i}_{layer}", [128, 128], F32)) for i in range(2)]
        b_dg = [Buf("dg0"), Buf("dg1")]
        kb.op('dve', lambda e: e.memset(ones[:], 1.0), writes=[b_ones])
        for r in range(1 + NS):
            for kc in range(32):
                d_ = kc % 2
                pi = (kc // 4) % 2
                kb.op('dve', lambda e, d_=d_, kc=kc, r=r: e.tensor_scalar(out=dg[d_][:], in0=ident[:], scalar1=modT[layer][:, 64 + kc, r:r + 1],
                                                                          scalar2=None, op0=ALU.mult),
                      reads=[b_ident, b_modT[layer]], writes=[b_dg[d_]])
                kb.op('pe', lambda e, d_=d_, kc=kc, pi=pi: e.matmul(psum[pi][:, (kc % 4) * 128:(kc % 4 + 1) * 128], lhsT=ones[:], rhs=dg[d_][:],
                                                                   start=True, stop=True),
                      reads=[b_ones, b_dg[d_]], writes=[b_ps[pi]])
                if kc % 4 == 3:
                    kb.op('act', lambda e, kc=kc, pi=pi, r=r: e.copy(out=gbc[r][:, (kc // 4) * 512:(kc // 4 + 1) * 512], in_=psum[pi][:, :]),
                          reads=[b_ps[pi]], writes=[b_gbc[r]])
        return gbc, b_gbc

    class ResidEpi:
        def __init__(self, es, gbc, b_gbc, resid_fn, out_fn, CW):
            self.gbc = gbc; self.b_gbc = b_gbc; self.resid_fn = resid_fn; self.out_fn = out_fn
            UID[0] += 1
            self.rs = [es.enter_context(nc.sbuf_tensor(f"rs{i}_{CW}_{UID[0]}", [128, CW], F32)) for i in range(3)]
            self.ev = [es.enter_context(nc.sbuf_tensor(f"rev{i}_{CW}_{UID[0]}", [128, CW], F32)) for i in range(3)]
            self.b_rs = [Buf(f"rs{i}") for i in range(3)]; self.b_ev = [Buf(f"rev{i}") for i in range(3)]
            for i in range(3):
                kb.op('dve', lambda e, i=i: e.memset(self.rs[i][:], 0.0), writes=[self.b_rs[i]])
            self.i = 0
        def __call__(self, t, c0, cw, pi):
            sl = self.i % 3; self.i += 1
            r = 0 if t < 16 else 1 + (t - 16)
            rs = self.rs[sl]; ev = self.ev[sl]; brs = self.b_rs[sl]; bev = self.b_ev[sl]
            src, nrow, sbufs = self.resid_fn(t, c0, cw)
            kb.dma('sp', rs[0:nrow, 0:cw], src, brs, reads=sbufs, writes=[brs])
            kb.op('dve', lambda e: e.tensor_tensor(out=ev[:, 0:cw], in0=psum[pi][:, 0:cw], in1=self.gbc[r][:, c0:c0 + cw], op=ALU.mult),
                  reads=[b_ps[pi], self.b_gbc[r]], writes=[bev])
            kb.op('dve', lambda e: e.tensor_tensor(out=ev[:, 0:cw], in0=ev[:, 0:cw], in1=rs[:, 0:cw], op=ALU.add),
                  reads=[bev, brs], writes=[bev])
            dst, nrow_o, obuf = self.out_fn(t, c0, cw)
            kb.dma('sp', dst, ev[0:nrow_o, 0:cw], bev, reads=[bev], writes=[obuf])

    def load_mixed(t, xt0, b_xt):
        kb.dma('sp', xt0[:], mixed_d[t * 128:(t + 1) * 128, :], b_xt, reads=[mixed_b[t]], writes=[b_xt])

    def resid0(t, c0, cw):
        if t < 16:
            return xp[t * 128:(t + 1) * 128, c0:c0 + cw], 128, []
        return xs[(t - 16) * 8:(t - 15) * 8, c0:c0 + cw], 8, []

    def out_x1(t, c0, cw):
        return x1_d[t * 128:(t + 1) * 128, c0:c0 + cw], 128, x1_b[t]

    with ExitStack() as esD:
        gbc0, b_gbc0 = make_gate_bc(0, esD)
        epiD = ResidEpi(esD, gbc0, b_gbc0, resid0, out_x1, 256)
        phase_mm(32, 256, [list(range(0, 6)), list(range(6, 12)), list(range(12, 18))], load_mixed, None, w_out0, D, epiD)
    if STAGE < 5:
        kb.finish()
        return nc
    w_in1 = dt_in("w_in1", [D, R_IN]); w_out1 = dt_in("w_out1", [2 * D, D])
    cosT = dt_in("cosT", [NTT * 128, 128]); sinT = dt_in("sinT", [NTT * 128, 128])
    dec_d = dt_in("dec", [2, 128, 3, 16]); cm_d = dt_in("cm", [128, 128])
    state_ret = dt_in("state_ret", [NS * 16, 256, 512])

    def load_x1(t, xt0, b_xt):
        kb.dma('sp', xt0[:], x1_d[t * 128:(t + 1) * 128, :], b_xt, reads=[x1_b[t]], writes=[b_xt])

    with ExitStack() as esE:
        epiE = StoreEpi(esE, proj1, proj1_b, 512)
        phase_mm(32, 512, [list(range(0, 8)) + [16], list(range(8, 16)) + [17]], load_x1, 1, w_in1, R_IN, epiE)
    if STAGE < 6:
        kb.finish()
        return nc

    GAM = [1.0 - 2.0 ** (-5 - h) for h in range(16)]
    with ExitStack() as esF:
        SBf = lambda name, shape, dt=F32: esF.enter_context(nc.sbuf_tensor(name, list(shape), dt))
        S = SBf("S_ret", [128, 16, 2, 512]); b_S = [Buf(f"S{h}") for h in range(16)]
        Sbh = [SBf(f"Sbh{i}", [128, 2, 512], BF16) for i in range(2)]; b_Sbh = [Buf("Sbh0"), Buf("Sbh1")]
        xin = SBf("xin", [128, D]); b_xin = Buf("xin")
        rot = SBf("rot", [128, 16, 2, 128]); b_rot = Buf("rot")
        t1 = SBf("t1", [128, 16, 128]); b_t1 = Buf("t1")
        kk = SBf("kk", [128, 16, 256], BF16); b_kk = Buf("kk")
        qdT = SBf("qdT", [128, 16, 2, 128], BF16); b_qdT = Buf("qdT")
        kdT = SBf("kdT", [128, 16, 2, 128], BF16); b_kdT = Buf("kdT")
        vb = SBf("vb", [128, 2 * D], BF16); b_vb = Buf("vb")
        gh = SBf("gh", [128, D]); b_gh = Buf("gh")
        osb = SBf("osb", [128, 8, 512], BF16); b_osb = Buf("osb")
        cs = SBf("cs", [128, 2, 128]); b_cs = Buf("cs")
        decs = SBf("decs", [128, 2, 3, 16]); b_decs = Buf("decs")
        cms = SBf("cms", [128, 128]); b_cms = Buf("cms")
        AT = SBf("AT", [128, 4, 128], BF16); b_AT = Buf("AT")
        ssq = SBf("ssq", [128, 16]); rstd = SBf("rstd", [128, 16]); b_ssq = Buf("ssq")
        fjunk = SBf("fjunk", [128, 512], BF16); b_fjunk = Buf("fjunk")
        kb.dma('sp', decs[:], dec_d.ap().rearrange("c p k h -> p c k h"), b_decs, writes=[b_decs])
        kb.dma('sp', cms[:], cm_d.ap(), b_cms, writes=[b_cms])
        for h in range(16):
            kb.op('dve', lambda e, h=h: e.memset(S[:, h, :, :], 0.0), writes=[b_S[h]])

        def ret_chunk(t, nrow, ci, C):
            rows = slice(t * 128, t * 128 + nrow)
            P = slice(0, nrow)
            kb.dma('sp', cs[P, 0, :], cosT[rows, :], b_cs, writes=[b_cs])
            kb.dma('sp', cs[P, 1, :], sinT[rows, :], b_cs, writes=[b_cs])
            kb.dma('pool', vb[P, :], proj1[rows, 8192:16384], b_vb, reads=[proj1_b[t]], writes=[b_vb])
            cosb = cs[P, 0, :].unsqueeze(1).to_broadcast([nrow, 16, 128])
            sinb = cs[P, 1, :].unsqueeze(1).to_broadcast([nrow, 16, 128])
            for which in range(2):
                kb.dma('sp', xin[P, :], proj1[rows, which * 4096:(which + 1) * 4096], b_xin, reads=[proj1_b[t]], writes=[b_xin])
                x4 = xin[P, :].rearrange("p (h two d) -> p h two d", h=16, two=2)
                x1 = x4[:, :, 0, :]; x2 = x4[:, :, 1, :]
                kb.op('dve', lambda e, x1=x1: e.tensor_tensor(out=rot[P, :, 0, :], in0=x1, in1=cosb, op=ALU.mult), reads=[b_xin, b_cs], writes=[b_rot])
                kb.op('dve', lambda e, x2=x2: e.tensor_tensor(out=t1[P, :, :], in0=x2, in1=sinb, op=ALU.mult), reads=[b_xin, b_cs], writes=[b_t1])
                kb.op('dve', lambda e: e.tensor_tensor(out=rot[P, :, 0, :], in0=rot[P, :, 0, :], in1=t1[P, :, :], op=ALU.subtract), reads=[b_rot, b_t1], writes=[b_rot])
                kb.op('dve', lambda e, x1=x1: e.tensor_tensor(out=rot[P, :, 1, :], in0=x1, in1=sinb, op=ALU.mult), reads=[b_xin, b_cs], writes=[b_rot])
                kb.op('dve', lambda e, x2=x2: e.tensor_tensor(out=t1[P, :, :], in0=x2, in1=cosb, op=ALU.mult), reads=[b_xin, b_cs], writes=[b_t1])
                kb.op('dve', lambda e: e.tensor_tensor(out=rot[P, :, 1, :], in0=rot[P, :, 1, :], in1=t1[P, :, :], op=ALU.add), reads=[b_rot, b_t1], writes=[b_rot])
                r3 = rot[P, :, :, :].rearrange("p h two d -> p h (two d)")
                if which == 1:
                    kb.op('dve', lambda e, r3=r3: e.tensor_tensor(out=kk[P, :, :], in0=r3, in1=decs[P, ci, 2, :].unsqueeze(2).to_broadcast([nrow, 16, 256]), op=ALU.mult),
                          reads=[b_rot, b_decs], writes=[b_kk])
                kb.op('dve', lambda e, r3=r3, which=which: e.tensor_tensor(out=r3, in0=r3, in1=decs[P, ci, which, :].unsqueeze(2).to_broadcast([nrow, 16, 256]), op=ALU.mult),
                      reads=[b_rot, b_decs], writes=[b_rot])
                dstT = qdT if which == 0 else kdT
                b_dstT = b_qdT if which == 0 else b_kdT
                rflat = rot[P, :, :, :].rearrange("p h two d -> p (h two d)")
                for c4 in range(8):
                    pi = c4 % 2
                    for kk_ in range(4):
                        c = c4 * 4 + kk_
                        kb.op('pe', lambda e, c=c, kk_=kk_, pi=pi, rflat=rflat: e.transpose(
                            psum[pi][:, kk_ * 128:kk_ * 128 + nrow], rflat[:, c * 128:(c + 1) * 128], ident[P, P]),
                            reads=[b_rot, b_ident], writes=[b_ps[pi]], inc=(kk_ == 3))
                    dv = dstT[:, 2 * c4:2 * c4 + 2, :, 0:nrow].rearrange("p h two i -> p (h two) i")
                    sv = psum[pi][:, 0:512].rearrange("p (c i) -> p c i", c=4)[:, :, 0:nrow]
                    if c4 % 2 == 0:
                        kb.op('act', lambda e, dv=dv, sv=sv: e.copy(out=dv, in_=sv), reads=[b_ps[pi]], writes=[b_dstT])
                    else:
                        kb.op('dve', lambda e, dv=dv, sv=sv: e.tensor_copy(out=dv, in_=sv), reads=[b_ps[pi]], writes=[b_dstT])
            for half in range(2):
                kb.dma('sp', gh[P, :], proj1[rows, 16384 + half * 4096:16384 + (half + 1) * 4096], b_gh, reads=[proj1_b[t]], writes=[b_gh])
                kb.op('act', lambda e: e.activation(out=gh[P, :], in_=gh[P, :], func=AF.Silu), reads=[b_gh], writes=[b_gh])
                for h4 in range(2):
                    hs4 = [half * 8 + h4 * 4 + j for j in range(4)]
                    pa = 2
                    for j, h in enumerate(hs4):
                        for dt_ in range(2):
                            kb.op('pe', lambda e, j=j, h=h, dt_=dt_: e.matmul(psum[pa][P, j * 128:j * 128 + nrow], lhsT=kdT[:, h, dt_, 0:nrow], rhs=qdT[:, h, dt_, 0:nrow],
                                                                            start=(dt_ == 0), stop=(dt_ == 1)),
                                  reads=[b_kdT, b_qdT], writes=[b_ps[pa]], inc=(j == 3 and dt_ == 1))
                    sv = psum[pa][P, 0:512].rearrange("p (c i) -> p c i", c=4)[:, :, 0:nrow]
                    kb.op('dve', lambda e, sv=sv: e.tensor_tensor(out=AT[P, :, 0:nrow], in0=sv, in1=cms[P, 0:nrow].unsqueeze(1).to_broadcast([nrow, 4, nrow]), op=ALU.mult),
                          reads=[b_ps[pa], b_cms], writes=[b_AT])
                    for j, h in enumerate(hs4):
                        hl = h - half * 8
                        sb_i = h % 2
                        kb.op('act', lambda e, h=h, sb_i=sb_i: e.copy(out=Sbh[sb_i][:], in_=S[:, h, :, :]), reads=[b_S[h]], writes=[b_Sbh[sb_i]])
                        po = 3 + (h % 2)
                        kb.op('pe', lambda e, j=j, h=h, po=po: e.matmul(psum[po][P, :], lhsT=AT[P, j, 0:nrow], rhs=vb[P, h * 512:(h + 1) * 512], start=True, stop=False),
                              reads=[b_AT, b_vb], writes=[b_ps[po]], inc=False)
                        for dt_ in range(2):
                            kb.op('pe', lambda e, h=h, po=po, dt_=dt_, sb_i=sb_i: e.matmul(psum[po][P, :], lhsT=qdT[:, h, dt_, 0:nrow], rhs=Sbh[sb_i][:, dt_, :],
                                                                                        start=False, stop=(dt_ == 1)),
                                  reads=[b_qdT, b_Sbh[sb_i]], writes=[b_ps[po]], inc=(dt_ == 1))
                        kb.op('act', lambda e, h=h, po=po: e.activation(out=fjunk[P, :], in_=psum[po][P, :], func=AF.Square, accum_out=ssq[P, h:h + 1]),
                              reads=[b_ps[po]], writes=[b_fjunk, b_ssq])
                        kb.op('dve', lambda e, hl=hl, po=po: e.tensor_copy(out=osb[P, hl, :], in_=psum[po][P, :]), reads=[b_ps[po], b_fjunk], writes=[b_osb])
                        for dt_ in range(2):
                            pu = 5 + dt_
                            kb.op('pe', lambda e, h=h, dt_=dt_, pu=pu: e.matmul(psum[pu][:, :], lhsT=kk[P, h, dt_ * 128:(dt_ + 1) * 128], rhs=vb[P, h * 512:(h + 1) * 512],
                                                                               start=True, stop=True),
                                  reads=[b_kk, b_vb], writes=[b_ps[pu]])
                            kb.op('dve', lambda e, h=h, dt_=dt_, pu=pu: e.scalar_tensor_tensor(out=S[:, h, dt_, :], in0=S[:, h, dt_, :], scalar=float(GAM[h] ** C),
                                                                                              in1=psum[pu][:, :], op0=ALU.mult, op1=ALU.add),
                                  reads=[b_ps[pu], b_S[h], b_Sbh[sb_i]], writes=[b_S[h]])
                hsl = slice(half * 8, half * 8 + 8)
                kb.op('dve', lambda e, hsl=hsl: e.tensor_scalar(out=rstd[P, hsl], in0=ssq[P, hsl], scalar1=1.0 / 512, scalar2=EPS, op0=ALU.mult, op1=ALU.add),
                      reads=[b_ssq], writes=[b_ssq])
                kb.op('act', lambda e, hsl=hsl: e.activation(out=rstd[P, hsl], in_=rstd[P, hsl], func=AF.Sqrt), reads=[b_ssq], writes=[b_ssq])
                kb.op('dve', lambda e, hsl=hsl: e.reciprocal(out=rstd[P, hsl], in_=rstd[P, hsl]), reads=[b_ssq], writes=[b_ssq])
                for hl in range(8):
                    kb.op('dve', lambda e, hl=hl, half=half: e.scalar_tensor_tensor(
                        out=gh[P, hl * 512:(hl + 1) * 512], in0=osb[P, hl, :], scalar=rstd[P, half * 8 + hl:half * 8 + hl + 1],
                        in1=gh[P, hl * 512:(hl + 1) * 512], op0=ALU.mult, op1=ALU.mult),
                        reads=[b_osb, b_ssq, b_gh], writes=[b_gh])
                kb.dma('sp', og_d[rows, half * 4096:(half + 1) * 4096], gh[P, :], b_gh, reads=[b_gh], writes=[og_b[t]])

        b_retout = Buf("retout")
        for t in range(16):
            ret_chunk(t, 128, 0, 128)
        kb.dma('sp', ret_p.ap().rearrange("h (dt p) v -> p h dt v", p=128), S[:], b_S[0], reads=b_S, writes=[b_retout])
        for si in range(NS):
            kb.dma('sp', S[:], state_ret[si * 16:(si + 1) * 16].rearrange("h (dt p) v -> p h dt v", p=128), b_S[0], reads=[b_retout], writes=b_S)
            ret_chunk(16 + si, 8, 1, 8)
            kb.dma('sp', ret_s[si * 16:(si + 1) * 16].rearrange("h (dt p) v -> p h dt v", p=128), S[:], b_S[0], reads=b_S, writes=[b_retout])
        kb.barrier()

    if STAGE < 7:
        kb.finish()
        return nc
    def load_og(t, xt0, b_xt):
        kb.dma('sp', xt0[:], og_d[t * 128:(t + 1) * 128, :], b_xt, reads=[og_b[t]], writes=[b_xt])

    def resid1(t, c0, cw):
        return x1_d[t * 128:t * 128 + (128 if t < 16 else 8), c0:c0 + cw], (128 if t < 16 else 8), [x1_b[t]]

    b_yout = Buf("yout")

    def out_y(t, c0, cw):
        if t < 16:
            return y_p[t * 128:(t + 1) * 128, c0:c0 + cw], 128, b_yout
        return y_s[(t - 16) * 8:(t - 15) * 8, c0:c0 + cw], 8, b_yout

    with ExitStack() as esG:
        gbc1, b_gbc1 = make_gate_bc(1, esG)
        epiG = ResidEpi(esG, gbc1, b_gbc1, resid1, out_y, 256)
        phase_mm(64, 256, [[2 * i, 2 * i + 1] for i in range(9)], load_og, None, w_out1, D, epiG)
    kb.finish()
    return nc


_PROG = None
N_CORES = 4


def _t5_bucket(dist):
    n = np.maximum(dist, 0)
    logv = np.log(np.maximum(n, 1).astype(np.float32) / np.float32(16)) / np.float32(np.log(1024 / 16))
    large = np.minimum(16 + (logv * np.float32(16)).astype(np.int32), 31)
    return np.where(n < 16, n, large)


def _host_constants():
    bf = ml_dtypes.bfloat16
    c = {}
    c["ident"] = np.eye(128, dtype=np.float32)
    c["jmat"] = np.ascontiguousarray(np.eye(128, dtype=np.float32)[::-1])
    d = np.arange(TABL, dtype=np.int64) - OFF
    bucket = _t5_bucket(d.astype(np.int32))
    for name, ok in (("oh_c", d >= 0), ("oh_w", (d >= 0) & (d < 512))):
        row = np.where(ok, bucket, 32)
        oh = np.zeros((33, TABL), np.float32)
        oh[row, np.arange(TABL)] = 1.0
        c[name] = oh.astype(bf)
    MT = np.zeros((127, 33), np.float32)
    for j in range(32):
        for cc, w in ((4 * j - 1, 1), (4 * j, 2), (4 * j + 1, 2), (4 * j + 2, 2), (4 * j + 3, 1)):
            if 0 <= cc < 127:
                MT[126 - cc, j] = w
    MT[:, 32] = 1.0
    c["MT1"] = MT.astype(bf)
    E = np.zeros((32, 16, 128), np.float32)
    for kt in range(16):
        for kp in range(128):
            E[(128 * kt + 127 - kp) // 64, kt, kp] = 1.0
    c["Eall"] = E.astype(bf)
    qpos = np.arange(2048)
    jj = np.arange(32)
    cur = qpos // 64
    forced = (jj[None] == 0) | (jj[None] == cur[:, None]) | (jj[None] == cur[:, None] - 1)
    valid = jj[None] * 64 <= qpos[:, None]
    vnf = (valid & ~forced).astype(np.float32)
    lay = lambda a: np.ascontiguousarray(a.reshape(16, 128, 32).transpose(1, 0, 2))
    c["vnf"] = lay(vnf); c["vnfm1"] = lay(vnf - 1.0); c["forced"] = lay(forced.astype(np.float32))
    pos = np.zeros(NTT * 128, np.float32)
    pos[:2048] = np.arange(2048, dtype=np.float32)
    for si in range(NS):
        pos[(16 + si) * 128:(16 + si) * 128 + 8] = 16384 + np.arange(8, dtype=np.float32)
    inv = np.power(np.float32(10000.0), -np.arange(128, dtype=np.float32) / np.float32(128)).astype(np.float32)
    ang = (pos[:, None] * inv[None, :]).astype(np.float32)
    c["cosT"] = np.cos(ang).astype(np.float32); c["sinT"] = np.sin(ang).astype(np.float32)
    gam = 1.0 - np.exp2(-5.0 - np.arange(16))
    dec = np.zeros((2, 128, 3, 16), np.float64)
    for ci, C in enumerate((128, 8)):
        i = np.arange(128, dtype=np.float64)[:, None]
        dec[ci, :, 0, :] = gam[None] ** (i + 1)
        dec[ci, :, 1, :] = gam[None] ** (-(i + 1)) / 16.0
        dec[ci, :, 2, :] = gam[None] ** (C - 1 - i) / 16.0
        if C < 128:
            dec[ci, C:, :, :] = 0.0
    c["dec"] = dec.astype(np.float32)
    c["cm"] = np.triu(np.ones((128, 128), np.float32))
    c["iota_p"] = (2.0 * np.arange(128, dtype=np.float32)).reshape(128, 1)
    MTs = np.zeros((128, 8, 257), np.float32)
    for j in range(257):
        for cc, w in ((4 * j - 1, 1), (4 * j, 2), (4 * j + 1, 2), (4 * j + 2, 2), (4 * j + 3, 1)):
            if 0 <= cc < 1023:
                bt, cl = divmod(cc, 128)
                MTs[127 - cl, bt, j] = w
    c["MTs"] = MTs.astype(bf)
    E2 = np.zeros((2, 128), np.float32)
    E2[0, 64:] = 1.0
    E2[1, :64] = 1.0
    c["E2"] = E2.astype(bf)
    jj = np.arange(257)
    forced_s = ((jj == 0) | (jj == 256) | (jj == 255)).astype(np.float32)
    vnf_s = 1.0 - forced_s
    c["vnf_s"] = np.ascontiguousarray(np.broadcast_to(np.stack([vnf_s, vnf_s - 1.0])[None], (32, 2, 257))).astype(np.float32)
    c["forced_s"] = np.ascontiguousarray(np.broadcast_to(forced_s[None], (32, 257))).astype(np.float32)
    return c


def kernel(**inp):
    global _PROG
    f = lambda a: np.ascontiguousarray(np.asarray(a), dtype=np.float32)
    if _PROG is None:
        _PROG = build_program()
    nc = _PROG
    x_prompt = np.asarray(inp["x_prompt"]); x_sample = np.asarray(inp["x_sample"])
    c_prompt = np.asarray(inp["c_prompt"]); c_sample = np.asarray(inp["c_sample"])
    ada_b = np.asarray(inp["ada_b"]); norm_g = np.asarray(inp["norm_g"])
    shared = dict(_host_constants())
    shared["ada_w"] = f(np.asarray(inp["ada_w"])[0:N_ADA])
    shared["ada_bT"] = f(ada_b.reshape(2, 96, 128).transpose(0, 2, 1))
    shared["norm_gT"] = f(norm_g.reshape(2, 32, 128).transpose(0, 2, 1))
    shared["w_in0"] = f(np.asarray(inp["a_w_in"])[0])
    shared["w_out0"] = f(np.asarray(inp["a_w_out"])[0])
    shared["w_in1"] = f(np.asarray(inp["r_w_in"])[0])
    shared["w_out1"] = f(np.asarray(inp["r_w_out"])[0])
    shared["qkg_bc"] = f(np.broadcast_to(np.asarray(inp["a_qk_g"])[0][None], (128, 4, 128)))
    shared["rel_bias"] = f(inp["rel_bias"])
    shared["peT"] = f(np.asarray(inp["a_cmp_pe"])[0].transpose(2, 0, 1))
    shared["cmp_w1"] = f(np.asarray(inp["a_cmp_w1"])[0])
    shared["cmp_w2"] = f(np.asarray(inp["a_cmp_w2"])[0])
    shared["cache_kv"] = f(np.asarray(inp["cache_nsa_kv"])[0].reshape(2 * 1280 * 128, 1024))
    page_table = np.ascontiguousarray(np.asarray(inp["page_table"]), dtype=np.int32)
    cache_win = np.asarray(inp["cache_nsa_win"])[0]
    state_ret = np.asarray(inp["state_ret"])[0]
    in_maps = []
    for c in range(N_CORES):
        sidx = [c + 4 * si for si in range(NS)]
        cT = np.stack([c_prompt[c].reshape(32, 128).T] + [c_sample[s].reshape(32, 128).T for s in sidx], axis=-1)
        m = dict(shared)
        m.update({"xp": f(x_prompt[c]), "xs": f(np.concatenate([x_sample[s] for s in sidx], axis=0)), "cT": f(cT),
                  "page_tab": np.ascontiguousarray(page_table[sidx]),
                  "cache_win": f(np.concatenate([cache_win[s].reshape(512, 1024) for s in sidx], axis=0)),
                  "state_ret": f(np.concatenate([state_ret[s] for s in sidx], axis=0))})
        in_maps.append({k: m[k] for k in INPUT_NAMES})
    res = run_bass_kernel_spmd(nc, in_maps, core_ids=list(range(N_CORES))).results
    res = list(res) + [res[0]] * (4 - len(res))
    global LAST_RES
    LAST_RES = res
    samp = lambda name, rows: np.stack([res[s % 4][name].reshape(NS, rows, -1)[s // 4] for s in range(8)])
    y_p = np.stack([res[b]["y_p"] for b in range(4)]).reshape(4, SEQ, D)
    y_s = samp("y_s", 8).reshape(8, 8, D)
    kvr_p = np.stack([res[b]["kvr_p"] for b in range(4)]).reshape(1, 4, SEQ, 4, 4, 128)
    kvr_s = samp("kvr_s", 8).reshape(1, 8, 8, 4, 4, 128)
    win_p = np.stack([res[b]["win_p"] for b in range(4)]).reshape(1, 4, 512, 2, 4, 128)
    win_s = samp("win_s", 512).reshape(1, 8, 512, 2, 4, 128)
    ret_p = np.stack([res[b]["ret_p"] for b in range(4)]).reshape(1, 4, 16, 256, 512)
    ret_s = samp("ret_s", 16 * 256).reshape(1, 8, 16, 256, 512)
    return (y_p, y_s, kvr_p, kvr_s, win_p, win_s, ret_p, ret_s)


LAST_RES = None
```
